# Optimizing a Trainium2 kernel written in Bass

```python
import math
import jax, jax.numpy as jnp
from jax import lax
import numpy as np

D_MODEL = 1024
BATCH = 2
SEQ = 8192
DEPTH = 4

GRID_W = 64
ROPE_THETA = 10000.0
Q_BLOCK = 128
EPS = 1e-6

GQA_HEADS = 8
GQA_KV_HEADS = 2
GQA_GROUP = GQA_HEADS // GQA_KV_HEADS
GQA_HEAD_DIM = D_MODEL // 16
GQA_Q_W = GQA_HEADS * GQA_HEAD_DIM
GQA_KV_W = GQA_KV_HEADS * GQA_HEAD_DIM

MLA_HEADS = 8
MLA_NOPE_DIM = D_MODEL // 16
MLA_ROPE_DIM = D_MODEL // 32
MLA_V_DIM = D_MODEL // 16
MLA_QK_DIM = MLA_NOPE_DIM + MLA_ROPE_DIM
MLA_Q_RANK = (3 * D_MODEL) // 8
MLA_KV_RANK = D_MODEL // 4
MLA_OUT_W = MLA_HEADS * MLA_V_DIM

D_FF = 4 * D_MODEL

SPLIT_SIZES = (GQA_Q_W, GQA_KV_W, GQA_KV_W, MLA_Q_RANK, MLA_KV_RANK, MLA_ROPE_DIM, 2 * D_MODEL)
IN_W = sum(SPLIT_SIZES)
SPLIT_POINTS = [int(v) for v in np.cumsum(SPLIT_SIZES)[:-1]]

kernel_name = "hybrid_gqa_mla_sandwich_encoder"


def rmsnorm(x, g):
    xf = x.astype(jnp.float32)
    y = xf * lax.rsqrt(jnp.mean(xf * xf, axis=-1, keepdims=True) + EPS)
    return (y * g.astype(jnp.float32)).astype(x.dtype)


def axial_rope_tables(seq, rot_dim):
    rows = seq // GRID_W
    row = jnp.repeat(jnp.arange(rows, dtype=jnp.float32), GRID_W)
    col = jnp.tile(jnp.arange(GRID_W, dtype=jnp.float32), rows)
    half = rot_dim // 2
    inv = ROPE_THETA ** (-jnp.arange(0, half, 2, dtype=jnp.float32) / half)
    ar = row[:, None] * inv[None, :]
    ac = col[:, None] * inv[None, :]
    ang = jnp.concatenate([ar, ar, ac, ac], axis=-1)
    return jnp.cos(ang), jnp.sin(ang)


def apply_axial_rope(x, cos, sin):
    d = x.shape[-1]
    h = d // 2
    q = h // 2
    shape = (cos.shape[0],) + (1,) * (x.ndim - 3) + (d,)
    c = cos.reshape(shape).astype(x.dtype)
    s = sin.reshape(shape).astype(x.dtype)
    xr, xc = x[..., :h], x[..., h:]
    rot = lambda z: jnp.concatenate([-z[..., q:], z[..., :q]], axis=-1)
    x_rot = jnp.concatenate([rot(xr), rot(xc)], axis=-1)
    return x * c + x_rot * s


def blocked_attention(q, k, v, scale):
    b, s, hk, g, dk = q.shape
    dv = v.shape[-1]
    nb = s // Q_BLOCK
    qb = q.reshape(b, nb, Q_BLOCK, hk, g, dk).swapaxes(0, 1)

    def one_block(qblk):
        sc = jnp.einsum('bqhgd,bkhd->bhgqk', qblk, k).astype(jnp.float32) * scale
        p = jax.nn.softmax(sc, axis=-1).astype(v.dtype)
        return jnp.einsum('bhgqk,bkhd->bqhgd', p, v)

    out = lax.map(one_block, qb)
    return out.swapaxes(0, 1).reshape(b, s, hk * g * dv)


def token_mixer(u, w_in, b_gate, q_norm_g, k_norm_g, q_a_norm_g, kv_a_norm_g,
                w_q_up, w_kv_up, w_branch_a, w_branch_b, w_o,
                cos_a, sin_a, cos_b, sin_b):
    b, s, _ = u.shape
    z = u @ w_in
    qa, ka, va, cq, ckv, kr, gl = jnp.split(z, SPLIT_POINTS, axis=-1)

    qa = qa.reshape(b, s, GQA_HEADS, GQA_HEAD_DIM)
    ka = ka.reshape(b, s, GQA_KV_HEADS, GQA_HEAD_DIM)
    va = va.reshape(b, s, GQA_KV_HEADS, GQA_HEAD_DIM)
    qa = apply_axial_rope(rmsnorm(qa, q_norm_g), cos_a, sin_a)
    ka = apply_axial_rope(rmsnorm(ka, k_norm_g), cos_a, sin_a)
    qa = qa.reshape(b, s, GQA_KV_HEADS, GQA_GROUP, GQA_HEAD_DIM)
    ya = blocked_attention(qa, ka, va, 1.0 / math.sqrt(GQA_HEAD_DIM))

    qb = (rmsnorm(cq, q_a_norm_g) @ w_q_up).reshape(b, s, MLA_HEADS, MLA_QK_DIM)
    q_nope, q_rope = qb[..., :MLA_NOPE_DIM], qb[..., MLA_NOPE_DIM:]
    q_rope = apply_axial_rope(q_rope, cos_b, sin_b)
    kvb = (rmsnorm(ckv, kv_a_norm_g) @ w_kv_up).reshape(b, s, MLA_HEADS, MLA_NOPE_DIM + MLA_V_DIM)
    k_nope, vb = kvb[..., :MLA_NOPE_DIM], kvb[..., MLA_NOPE_DIM:]
    k_rope = apply_axial_rope(kr, cos_b, sin_b)
    k_rope = jnp.broadcast_to(k_rope[:, :, None, :], (b, s, MLA_HEADS, MLA_ROPE_DIM))
    qb = jnp.concatenate([q_nope, q_rope], axis=-1)[:, :, :, None, :]
    kb = jnp.concatenate([k_nope, k_rope], axis=-1)
    yb = blocked_attention(qb, kb, vb, 1.0 / math.sqrt(MLA_QK_DIM))

    gates = jax.nn.sigmoid((gl + b_gate).astype(jnp.float32)).astype(u.dtype)
    g_a, g_b = gates[..., :D_MODEL], gates[..., D_MODEL:]
    merged = g_a * (ya @ w_branch_a) + g_b * (yb @ w_branch_b)
    return merged @ w_o


def setup_inputs(seed: int = 0) -> dict:
    key = jax.random.key(seed)
    ks = jax.random.split(key, 20)
    f32 = jnp.float32

    def w(k, fan_in, fan_out):
        return jax.random.normal(k, (DEPTH, fan_in, fan_out), f32) * fan_in ** -0.5

    def gain(k, n):
        return 1.0 + 0.05 * jax.random.normal(k, (DEPTH, n), f32)

    return {
        "x": jax.random.normal(ks[0], (BATCH, SEQ, D_MODEL), f32),
        "w_in": w(ks[1], D_MODEL, IN_W),
        "b_gate": 0.1 * jax.random.normal(ks[2], (DEPTH, 2 * D_MODEL), f32),
        "q_norm_g": gain(ks[3], GQA_HEAD_DIM),
        "k_norm_g": gain(ks[4], GQA_HEAD_DIM),
        "q_a_norm_g": gain(ks[5], MLA_Q_RANK),
        "kv_a_norm_g": gain(ks[6], MLA_KV_RANK),
        "w_q_up": w(ks[7], MLA_Q_RANK, MLA_HEADS * MLA_QK_DIM),
        "w_kv_up": w(ks[8], MLA_KV_RANK, MLA_HEADS * (MLA_NOPE_DIM + MLA_V_DIM)),
        "w_branch_a": w(ks[9], GQA_Q_W, D_MODEL),
        "w_branch_b": w(ks[10], MLA_OUT_W, D_MODEL),
        "w_o": w(ks[11], D_MODEL, D_MODEL),
        "w_ffn_up": w(ks[12], D_MODEL, D_FF),
        "w_ffn_down": w(ks[13], D_FF, D_MODEL),
        "pre_mix_g": gain(ks[14], D_MODEL),
        "post_mix_g": gain(ks[15], D_MODEL),
        "pre_ffn_g": gain(ks[16], D_MODEL),
        "post_ffn_g": gain(ks[17], D_MODEL),
    }


def reference(x, w_in, b_gate, q_norm_g, k_norm_g, q_a_norm_g, kv_a_norm_g,
              w_q_up, w_kv_up, w_branch_a, w_branch_b, w_o, w_ffn_up, w_ffn_down,
              pre_mix_g, post_mix_g, pre_ffn_g, post_ffn_g):
    seq = x.shape[1]
    cos_a, sin_a = axial_rope_tables(seq, GQA_HEAD_DIM)
    cos_b, sin_b = axial_rope_tables(seq, MLA_ROPE_DIM)
    for l in range(DEPTH):
        u = rmsnorm(x, pre_mix_g[l])
        m = token_mixer(u, w_in[l], b_gate[l], q_norm_g[l], k_norm_g[l],
                        q_a_norm_g[l], kv_a_norm_g[l], w_q_up[l], w_kv_up[l],
                        w_branch_a[l], w_branch_b[l], w_o[l],
                        cos_a, sin_a, cos_b, sin_b)
        x = x + rmsnorm(m, post_mix_g[l])
        h = rmsnorm(x, pre_ffn_g[l]) @ w_ffn_up[l]
        f = jnp.square(jax.nn.relu(h)) @ w_ffn_down[l]
        x = x + rmsnorm(f, post_ffn_g[l])
    return x
```

```python
import numpy as np
import ml_dtypes
import concourse.bass as bass
import concourse.mybir as mybir
from concourse.bass_utils import run_bass_kernel_spmd
from contextlib import ExitStack

F32 = mybir.dt.float32
BF16 = mybir.dt.bfloat16
ALU = mybir.AluOpType
AF = mybir.ActivationFunctionType

D = 1024
SEQ = 8192
T = 2048
NB = 4
EPS = 1e-6
NCOLA = 2112
KVROWS = 544
FUSED = True


class Op:
    __slots__ = ("eng", "name", "kw", "deps", "signal", "sigval", "grp", "kind", "ep")

    def __init__(self, eng, name, kw, kind="c"):
        self.ep = 0
        self.eng = eng
        self.name = name
        self.kw = kw
        self.deps = []
        self.signal = False
        self.sigval = 0
        self.grp = None
        self.kind = kind


class Grp:
    __slots__ = ("sem", "final")

    def __init__(self, sem):
        self.sem = sem
        self.final = 0


class Prog:
    ENGS = ("pe", "act", "dve", "pool", "sp")

    def __init__(self, nc):
        self.nc = nc
        self.streams = {e: [] for e in self.ENGS}
        self.lastw = {}
        self.readers = {}
        self.bar = []
        self.dcount = {}
        self.dlast = {}
        self.cccount = 0
        self.epoch = 0

    def _deps(self, o, reads, writes):
        deps = []
        for k in reads:
            w = self.lastw.get(k)
            if w is not None:
                deps.append(w)
        for k in writes:
            w = self.lastw.get(k)
            if w is not None:
                deps.append(w)
            deps.extend(self.readers.get(k, ()))
        deps.extend(self.bar)
        o.deps = deps
        for k in reads:
            self.readers.setdefault(k, []).append(o)
        for k in writes:
            self.lastw[k] = o
            self.readers[k] = []

    def op(self, eng, name, reads=(), writes=(), **kw):
        o = Op(eng, name, kw)
        o.ep = self.epoch
        self._deps(o, reads, writes)
        self.streams[eng].append(o)
        return o

    def group(self, sem):
        g = Grp(sem)
        return g

    def dma(self, queue, g, out, in_, reads=(), writes=()):
        o = Op(queue, "dma_start", dict(out=out, in_=in_), kind="d")
        o.grp = g
        self._deps(o, reads, writes)
        prev = self.dlast.get(g.sem)
        if prev is not None and prev.grp is not g:
            o.deps.append(prev)
        self.dcount[g.sem] = self.dcount.get(g.sem, 0) + 16
        g.final = self.dcount[g.sem]
        self.dlast[g.sem] = o
        self.streams[queue].append(o)
        return o

    def collective(self, args, reads=(), writes=(), **kw):
        o = Op("pool", "collective_compute", kw, kind="cc")
        o.grp = args
        self._deps(o, reads, writes)
        self.cccount += 1
        o.sigval = self.cccount
        self.streams["pool"].append(o)
        return o

    def barrier(self):
        b = []
        for e in self.ENGS:
            for o in reversed(self.streams[e]):
                if o.kind == "c" or o.kind == "cc":
                    b.append(o)
                    break
        b.extend(self.dlast.values())
        self.bar = b

    def lower(self, stack):
        nc = self.nc
        for e in self.ENGS:
            for o in self.streams[e]:
                for d in o.deps:
                    if d.kind == "c" and not (d.eng == "pe" and o.eng == "pe"):
                        d.signal = True
        sems = {}
        for e in ("pe", "act", "dve", "pool"):
            cnts = {}
            for o in self.streams[e]:
                if o.kind == "c" and o.signal:
                    cnts[o.ep] = cnts.get(o.ep, 0) + 1
                    o.sigval = cnts[o.ep]
                    if (e, o.ep) not in sems:
                        sems[(e, o.ep)] = stack.enter_context(nc.semaphore("s_%s%d" % (e, o.ep)))
        sems["cc"] = stack.enter_context(nc.semaphore("s_cc"))
        dsems = {}
        for name in self.dcount:
            dsems[name] = stack.enter_context(nc.semaphore("d_" + name))
        block = stack.enter_context(nc.Block())

        def run(ename, eng):
            waited = {}
            for o in self.streams[ename]:
                need = {}
                for d in o.deps:
                    if d.kind == "d":
                        if o.kind == "d" and d.grp is o.grp:
                            continue
                        key, val = ("d", d.grp.sem), d.grp.final
                    elif d.kind == "cc":
                        key, val = ("cc",), d.sigval
                    else:
                        if d.eng == "pe" and ename == "pe":
                            continue
                        key, val = ("c", d.eng, d.ep), d.sigval
                    if need.get(key, 0) < val:
                        need[key] = val
                for key, val in need.items():
                    if waited.get(key, 0) >= val:
                        continue
                    waited[key] = val
                    if key[0] == "d":
                        eng.wait_ge(dsems[key[1]], val)
                    elif key[0] == "cc":
                        eng.wait_ge(sems["cc"], val)
                    else:
                        eng.wait_ge(sems[(key[1], key[2])], val)
                if o.name is None:
                    continue
                if o.kind == "cc":
                    ins = eng.collective_compute(*o.grp, **o.kw)
                else:
                    ins = getattr(eng, o.name)(**o.kw)
                if o.kind == "d":
                    ins.then_inc(dsems[o.grp.sem], 16)
                elif o.kind == "cc":
                    ins.then_inc(sems["cc"], 1)
                elif o.signal:
                    ins.then_inc(sems[(o.eng, o.ep)], 1)

        @block.tensor
        def _(eng):
            run("pe", eng)

        @block.scalar
        def _(eng):
            run("act", eng)

        @block.vector
        def _(eng):
            run("dve", eng)

        @block.gpsimd
        def _(eng):
            run("pool", eng)

        @block.sync
        def _(eng):
            run("sp", eng)


def build_program(depth, stop=None):
    nc = bass.Bass("TRN2", target_bir_lowering=False)
    P = Prog(nc)

    def din(name, shape, dt=F32):
        return nc.dram_tensor(name, shape, dt, kind="ExternalInput")

    x_in = din("x_in", [T, D])
    w_inA = din("w_inA", [depth, D, NCOLA])
    w_g = din("w_g", [depth, D, 2048])
    w_qup = din("w_qup", [depth, 384, 1536])
    w_kvup = din("w_kvup", [depth, 256, 1024])
    w_a = din("w_a", [depth, 512, 1024])
    w_b = din("w_b", [depth, 512, 1024])
    w_o = din("w_o", [depth, D, D])
    w_up = din("w_up", [depth, D, 4096])
    w_dn = din("w_dn", [depth, 4096, D])
    vecP = din("vecP", [depth, 128, 48])
    vecB = din("vecB", [depth, 2, D])
    tabA = din("tabA", [2, 128, T])
    tabB = din("tabB", [2, 32, T])
    cst = din("cst", [3, 128, 128], BF16)
    identd = din("identd", [128, 128], BF16)
    out = nc.dram_tensor("out", [T, D], F32, kind="ExternalOutput")

    x_cur = nc.dram_tensor("x_cur", [T, D], F32)
    uT_scr = nc.dram_tensor("uT_scr", [D, T], BF16)
    KVR = (256, 160, 128)
    kvf_own = [[nc.dram_tensor("kvf_own%d_%d" % (i, q), [KVR[q], T], BF16) for q in range(3)] for i in range(2)]
    kvf_all = [[nc.dram_tensor("kvf_all%d_%d" % (i, q), [4 * KVR[q], T], BF16) for q in range(3)] for i in range(2)]

    stack = ExitStack()
    NBF = 105472
    SB = stack.enter_context(nc.sbuf_tensor("sb", [128, NBF], BF16))
    PS = [stack.enter_context(nc.psum_tensor("ps%d" % i, [128, 1024], F32)) for i in range(4)]

    def v16(off, n, p0=0, p1=128):
        assert off % 4 == 0 and off + 2 * n <= NBF * 2, (off, n)
        return SB[p0:p1, off // 2: off // 2 + n]

    def v32(off, n, p0=0, p1=128):
        assert off % 4 == 0 and off + 4 * n <= NBF * 2, (off, n)
        return SB[p0:p1, off // 2: off // 2 + 2 * n].bitcast(F32)

    def bank(b, p0=0, p1=128):
        return PS[b // 2][p0:p1, (b % 2) * 512:(b % 2 + 1) * 512]

    K = 1024
    C_ONES = 0
    C_BLK = 512
    C_ID = 1024
    C_EXP = 1280
    C_VEC = 3328
    C_BH = 3520
    C_GB = 3584
    C_END = 12 * K
    ones_f = v16(C_ONES, 128)
    blk_f = v16(C_BLK, 128)
    ident = v16(C_ID, 128)
    expt = v32(C_EXP, 512)
    vec = v32(C_VEC, 48)
    bhalf = v32(C_BH, 16)
    gpost = [v32(C_GB + i * 4096, 1024) for i in range(2)]

    g0 = P.group("const")
    P.dma("sp", g0, ones_f, cst[0, :, :], writes=["ones"])
    P.dma("sp", g0, blk_f, cst[1, :, :], writes=["blk"])
    P.dma("sp", g0, ident, identd[:, :], writes=["ident"])
    P.op("dve", "memset", writes=["expt"], ap=expt, constant=-0.5)
    epsb = v32(C_GB + 8192, 16)
    P.op("dve", "memset", writes=["epsb"], ap=epsb, constant=EPS)

    O_Y = 12 * K
    O_QB = 44 * K
    O_QA = 76 * K

    def YA(p0, p1, pc, c0, c1):
        return SB[p0:p1, (O_Y // 2 + pc * T + c0):(O_Y // 2 + pc * T + c1)]

    def YB(p0, p1, pc, c0, c1):
        return SB[p0:p1, (O_Y // 2 + 4 * T + pc * T + c0):(O_Y // 2 + 4 * T + pc * T + c1)]

    def QB(p0, p1, h, c0, c1):
        return SB[p0:p1, (O_QB // 2 + h * T + c0):(O_QB // 2 + h * T + c1)]

    def QA(p0, p1, pc, c0, c1):
        return SB[p0:p1, (O_QA // 2 + pc * T + c0):(O_QA // 2 + pc * T + c1)]

    psrr = [0]

    def nbank():
        b = psrr[0] % 8
        psrr[0] += 1
        return b

    pprr = [0]

    def npair():
        p = pprr[0] % 4
        pprr[0] += 1
        return p

    def mm(outap, lhsT, rhs, start, stop, reads, writes):
        return P.op("pe", "matmul", reads=reads, writes=writes, out=outap, lhsT=lhsT, rhs=rhs, start=start, stop=stop)

    def rms_tokens(xb_f, xkey, nt, ss, mse, rstd, junk, eps, key):
        for j in range(nt):
            P.op("act", "activation", reads=[xkey], writes=[(key, "ss", j), (key, "junk")], out=junk, in_=xb_f[:, j, :], func=AF.Square,
                                                    accum_out=ss[:, j:j + 1])
        P.op("dve", "tensor_scalar", reads=[(key, "ss", j) for j in range(nt)], writes=[(key, "mse")], out=mse[:, 0:nt], in0=ss[:, 0:nt], scalar1=1.0 / D, scalar2=eps,
                                              op0=ALU.mult, op1=ALU.add)
        P.op("pool", "tensor_tensor", reads=[(key, "mse"), "expt"], writes=[(key, "rstd")], out=rstd[:, 0:nt], in0=mse[:, 0:nt], in1=expt[:, 0:nt], op=ALU.pow)

    for l in range(depth):
        P.epoch = l
        par = l % 2
        x_src = x_in if l == 0 else x_cur
        x_dst_final = out if (l == depth - 1) else x_cur
        P.barrier()
        A_T = 12 * K
        A_WA = 92 * K
        A_WQ = 126 * K
        A_XB = 136 * K
        A_UT = 168 * K
        A_TB = 184 * K
        A_UR = 192 * K
        A_ST = 196 * K
        Wa = v16(A_WA, 8 * NCOLA).rearrange("p (k n) -> p k n", k=8)
        Wq = v16(A_WQ, 3 * 1536).rearrange("p (k n) -> p k n", k=3)
        for q, (ca, cb) in enumerate(((1024, NCOLA), (0, 1024))):
            P.dma("pool", P.group("w%d" % q), Wa[:, :, ca:cb], w_inA[l].rearrange("(k p) n -> p k n", p=128)[:, :, ca:cb],
                  writes=[("Wa", q)])
        P.dma("pool", P.group("w2"), Wq, w_qup[l].rearrange("(k p) n -> p k n", p=128), writes=["Wq"])
        gv = P.group("vec")
        P.dma("sp", gv, vec, vecP[l, :, :], writes=["vec"])
        for i in range(2):
            P.dma("sp", gv, gpost[i], bass.AP(tensor=vecB, offset=(l * 2 + i) * D, ap=[[0, 128], [1, D]]),
                  writes=[("gpost", i)])
        P.op("dve", "tensor_scalar", reads=["vec"], writes=["bhalf"], out=bhalf, in0=vec[:, 25:41], scalar1=0.5, scalar2=None, op0=ALU.mult)
        t_ss = v32(A_T, 8)
        t_mse = v32(A_T + 64, 8)
        t_rstd = v32(A_T + 128, 8)
        t_junk = v16(A_T + 256, 1024)
        t_sq = [v16(A_T + 4 * K + i * 2 * K, 512) for i in range(3)]
        t_ms = [v32(A_T + 10 * K + i * 2 * K, 512) for i in range(2)]
        t_rs = [v32(A_T + 14 * K + i * 2 * K, 512) for i in range(2)]
        t_1 = [v32(A_T + 18 * K + i * 2 * K, 512) for i in range(2)]
        t_2 = [v32(A_T + 22 * K + i * 2 * K, 512) for i in range(2)]
        t_cqn = v16(A_T + 26 * K, 3 * 512).rearrange("p (c t) -> p c t", c=3)
        cnt = {"sq": 0, "ms": 0, "t": 0}
        cosA = v32(A_TB, 512)
        sinA = v32(A_TB + 2 * K, 512)
        cosB = lambda p0, p1: v32(A_TB + 4 * K, 512, p0, p1)
        sinB = lambda p0, p1: v32(A_TB + 6 * K, 512, p0, p1)
        st_k = v16(A_ST, 512)
        st_ckv = v16(A_ST + 1 * K, 2 * 512).rearrange("p (c t) -> p c t", c=2)
        st_kr = v16(A_ST + 3 * K, 512, 0, 32)
        st_v = v16(A_ST + 4 * K, 4 * 128).rearrange("p (j c) -> p j c", j=4)
        kvo = kvf_own[par]
        kva = kvf_all[par]

        def a_bufs(tb):
            xb = v32(A_XB + (tb % 2) * 16 * K, 4 * 1024).rearrange("p (j d) -> p j d", j=4)
            uT = v16(A_UT + (tb % 2) * 8 * K, 8 * 512).rearrange("p (k t) -> p k t", k=8)
            uk = ("uT", tb % 2)
            ukeys = [(uk, j, c) for j in range(4) for c in range(8)]
            return xb, ("xb", tb % 2), uT, uk, ukeys

        def a_load(tb):
            xb, xk, uT, uk, ukeys = a_bufs(tb)
            P.dma("sp", P.group("xb%d" % (tb % 2)), xb, x_src[tb * 512:(tb + 1) * 512, :].rearrange("(j p) d -> p j d", p=128),
                  reads=[("xcur", tb)], writes=[xk])

        def a_tables(tb):
            c0, c1 = tb * 512, (tb + 1) * 512
            gt = P.group("tab")
            P.dma("sp", gt, cosA, tabA[0, :, c0:c1], writes=["tab"])
            P.dma("sp", gt, sinA, tabA[1, :, c0:c1], writes=["tab"])
            for (p0, p1) in ((0, 32), (64, 96)):
                P.dma("sp", gt, cosB(p0, p1), tabB[0, :, c0:c1], writes=["tab"])
                P.dma("sp", gt, sinB(p0, p1), tabB[1, :, c0:c1], writes=["tab"])

        def a_norm(tb):
            c0, c1 = tb * 512, (tb + 1) * 512
            xb, xk, uT, uk, ukeys = a_bufs(tb)
            rms_tokens(xb, xk, 4, t_ss, t_mse, t_rstd, t_junk, EPS, "nA")
            for j in range(4):
                ur = v16(A_UR + (j % 2) * 2 * K, 1024)
                urk = ("ur", j % 2)
                P.op("dve", "tensor_scalar", reads=[xk, ("nA", "rstd")], writes=[urk], out=ur, in0=xb[:, j, :],
                     scalar1=t_rstd[:, j:j + 1], scalar2=None, op0=ALU.mult)
                b = nbank()
                pT = bank(b).bitcast(BF16).rearrange("p (c t) -> p c t", t=128)
                for c in range(8):
                    P.op("pe", "transpose", reads=[urk, "ident"], writes=[("ps", b)], out=pT[:, c, :],
                         in_=ur[:, c * 128:(c + 1) * 128], identity=ident)
                for c in range(8):
                    P.op("dve", "tensor_scalar", reads=[("ps", b), "vec"], writes=[(uk, j, c)],
                         out=uT[:, c, j * 128:(j + 1) * 128], in0=pT[:, c, :], scalar1=vec[:, c:c + 1], scalar2=None,
                         op0=ALU.mult)
            P.dma("sp", P.group("ust%d" % (tb % 2)), uT_scr.ap().rearrange("(k p) t -> p k t", p=128)[:, :, c0:c1], uT,
                  reads=ukeys, writes=[("uscr", tb)])

        def a_helpers(tb):
            xb, xk, uT, uk, ukeys = a_bufs(tb)
            def proj(col0, M, b, p0=0):
                for k in range(8):
                    wk = ("Wa", 0 if col0 >= 1024 else 1)
                    mm(bank(b, p0, p0 + M), Wa[:, k, col0:col0 + M], uT[:, k, :], k == 0, k == 7,
                       reads=(ukeys if k in (0, 7) else []) + [wk], writes=[("ps", b)])

            def stat(sqs, blkones, scale_eps, nparts=128):
                b = nbank()
                n = len(sqs)
                for i, (sq, sk) in enumerate(sqs):
                    mm(bank(b), blkones, sq, i == 0, i == n - 1, reads=[sk, "ones", "blk"], writes=[("ps", b)])
                mi = cnt["ms"] % 2
                cnt["ms"] += 1
                ms, rs = t_ms[mi], t_rs[mi]
                P.op("act", "activation", reads=[("ps", b)], writes=[("ms", mi)], out=ms, in_=bank(b), func=AF.Sqrt,
                     scale=scale_eps[0], bias=epsb[:, 0:1])
                P.op("dve", "reciprocal", reads=[("ms", mi)], writes=[("rs", mi)], out=rs, in_=ms)
                return rs, ("rs", mi)

            def square_of(b):
                si = cnt["sq"] % 3
                cnt["sq"] += 1
                sq = t_sq[si]
                P.op("act", "activation", reads=[("ps", b)], writes=[("sq", si)], out=sq, in_=bank(b), func=AF.Square)
                return sq, ("sq", si)

            def rope_combine(b1, b2, gcol, gpcol, cs, sn, outap, outkey, rs=None, rsk=None, p0=0, p1=128):
                ti = cnt["t"] % 2
                cnt["t"] += 1
                a1 = v32(A_T + 18 * K + ti * 2 * K, 512, p0, p1)
                a2 = v32(A_T + 22 * K + ti * 2 * K, 512, p0, p1)
                if gcol is None:
                    P.op("dve", "tensor_tensor", reads=[("ps", b1), "tab"], writes=[("t1", ti)], out=a1, in0=bank(b1, p0, p1), in1=cs, op=ALU.mult)
                    P.op("dve", "tensor_tensor", reads=[("ps", b2), "tab"], writes=[("t2", ti)], out=a2, in0=bank(b2, p0, p1), in1=sn, op=ALU.mult)
                else:
                    P.op("dve", "scalar_tensor_tensor", reads=[("ps", b1), "tab", "vec"], writes=[("t1", ti)], out=a1, in0=bank(b1, p0, p1), scalar=vec[p0:p1, gcol:gcol + 1],
                                                                 in1=cs, op0=ALU.mult, op1=ALU.mult)
                    P.op("dve", "scalar_tensor_tensor", reads=[("ps", b2), "tab", "vec"], writes=[("t2", ti)], out=a2, in0=bank(b2, p0, p1), scalar=vec[p0:p1, gpcol:gpcol + 1],
                                                                 in1=sn, op0=ALU.mult, op1=ALU.mult)
                if rs is None:
                    P.op("pool", "tensor_tensor", reads=[("t1", ti), ("t2", ti)], writes=[outkey], out=outap, in0=a1, in1=a2, op=ALU.add)
                else:
                    P.op("pool", "tensor_tensor", reads=[("t1", ti), ("t2", ti)], writes=[("t1", ti)], out=a1, in0=a1, in1=a2, op=ALU.add)
                    P.op("pool", "tensor_tensor", reads=[("t1", ti), rsk], writes=[outkey], out=outap, in0=a1, in1=rs, op=ALU.mult)

            return proj, stat, square_of, rope_combine

        def a_kv(tb):
            c0, c1 = tb * 512, (tb + 1) * 512
            xb, xk, uT, uk, ukeys = a_bufs(tb)
            proj, stat, square_of, rope_combine = a_helpers(tb)
            gk = P.group("kvst")
            b1, b2 = nbank(), nbank()
            proj(1024, 128, b1)
            proj(1152, 128, b2)
            sq, sk = square_of(b1)
            rs, rsk = stat([(sq, sk)], blk_f, (1.0 / 64, EPS))
            rope_combine(b1, b2, 23, 24, cosA, sinA, st_k, ("st", "k"), rs, rsk)
            P.dma("sp", gk, kvo[1][0:128, c0:c1], st_k, reads=[("st", "k")], writes=[("kvown", par)])
            bs = [nbank() for _ in range(2)]
            sqs = []
            for c in range(2):
                proj(1664 + c * 128, 128, bs[c])
                sqs.append(square_of(bs[c]))
            rs, rsk = stat(sqs, ones_f, (1.0 / 256, EPS))
            for c in range(2):
                P.op("dve", "scalar_tensor_tensor", reads=[("ps", bs[c]), rsk, "vec"], writes=[("st", "ckv", c)], out=st_ckv[:, c, :], in0=bank(bs[c]),
                                                                       scalar=vec[:, 19 + c:20 + c], in1=rs,
                                                                       op0=ALU.mult, op1=ALU.mult)
            P.dma("sp", gk, kvo[0].ap()[0:256, c0:c1].rearrange("(c p) t -> p c t", p=128), st_ckv,
                  reads=[("st", "ckv", 0), ("st", "ckv", 1)], writes=[("kvown", par)])
            b1, b2 = nbank(), nbank()
            proj(1920, 32, b1)
            proj(1952, 32, b2)
            rope_combine(b1, b2, None, None, cosB(0, 32), sinB(0, 32), st_kr, ("st", "kr"), p0=0, p1=32)
            P.dma("sp", gk, kvo[1][128:160, c0:c1], st_kr, reads=[("st", "kr")], writes=[("kvown", par)])
            b = nbank()
            for j in range(4):
                for k in range(8):
                    mm(bank(b)[:, j * 128:(j + 1) * 128], uT[:, k, j * 128:(j + 1) * 128], Wa[:, k, 1984:2112],
                       k == 0, k == 7, reads=(ukeys if k in (0, 7) else []) + [("Wa", 0)], writes=[("ps", b)])
            P.op("act", "copy", reads=[("ps", b)], writes=[("st", "v")], out=st_v, in_=bank(b).rearrange("p (j c) -> p j c", j=4))
            vdst = bass.AP(tensor=kvo[2], offset=c0 * 128, ap=[[128, 128], [128 * 128, 4], [1, 128]])
            P.dma("sp", gk, vdst, st_v, reads=[("st", "v")], writes=[("kvown", par)])


        def a_q(tb):
            c0, c1 = tb * 512, (tb + 1) * 512
            xb, xk, uT, uk, ukeys = a_bufs(tb)
            proj, stat, square_of, rope_combine = a_helpers(tb)
            for pc in range(4):
                b1, b2 = nbank(), nbank()
                proj(pc * 128, 128, b1)
                proj(512 + pc * 128, 128, b2)
                sq, sk = square_of(b1)
                rs, rsk = stat([(sq, sk)], blk_f, (1.0 / 64, EPS))
                rope_combine(b1, b2, 21, 22, cosA, sinA, QA(0, 128, pc, c0, c1), ("QA", pc, tb), rs, rsk)
            bs = [nbank() for _ in range(3)]
            sqs = []
            for c in range(3):
                proj(1280 + c * 128, 128, bs[c])
                sqs.append(square_of(bs[c]))
            rs, rsk = stat(sqs, ones_f, (1.0 / 384, EPS))
            for c in range(3):
                P.op("dve", "scalar_tensor_tensor", reads=[("ps", bs[c]), rsk, "vec"], writes=[("cqn", c)], out=t_cqn[:, c, :], in0=bank(bs[c]),
                                                                       scalar=vec[:, 16 + c:17 + c], in1=rs,
                                                                       op0=ALU.mult, op1=ALU.mult)
            for h in range(8):
                b1, b2 = nbank(), nbank()
                for c in range(3):
                    mm(bank(b1, 0, 96), Wq[:, c, h * 192:h * 192 + 96], t_cqn[:, c, :], c == 0, c == 2,
                       reads=[("cqn", c), "Wq"], writes=[("ps", b1)])
                for c in range(3):
                    mm(bank(b2, 0, 96), Wq[:, c, h * 192 + 96:h * 192 + 192], t_cqn[:, c, :], c == 0, c == 2,
                       reads=[("cqn", c), "Wq"], writes=[("ps", b2)])
                P.op("act", "copy", reads=[("ps", b1)], writes=[("QB", h, tb, 0)], out=QB(0, 64, h, c0, c1), in_=bank(b1, 0, 64))
                rope_combine(b1, b2, None, None, cosB(64, 96), sinB(64, 96), QB(64, 96, h, c0, c1), ("QB", h, tb, 1),
                             p0=64, p1=96)

        def a_reload(tb):
            xb, xk, uT, uk, ukeys = a_bufs(tb)
            P.dma("sp", P.group("xb%d" % (tb % 2)), uT,
                  uT_scr.ap().rearrange("(k p) t -> p k t", p=128)[:, :, tb * 512:(tb + 1) * 512],
                  reads=[("uscr", tb)], writes=ukeys)

        if stop != "A0":
            a_load(0)
            a_norm(0)
            for tb in range(NB):
                if tb + 1 < NB:
                    a_load(tb + 1)
                a_tables(tb)
                a_kv(tb)
                if tb + 1 < NB:
                    a_norm(tb + 1)
            if stop not in ("A", "A1"):
                for q in range(3):
                    P.collective(("AllGather", ALU.bypass), reads=[("kvown", par)], writes=[("kvall", par, q)],
                                 replica_groups=[[0, 1, 2, 3], [4, 5, 6, 7]],
                                 ins=[kvf_own[par][q].ap().opt()], outs=[kva[q].ap().opt()])
            a_reload(0)
            for tb in range(NB):
                if tb + 1 < NB:
                    a_reload(tb + 1)
                a_tables(tb)
                a_q(tb)

        if stop in ("A", "A0", "A1"):
            break
        if stop == "AG":
            break
        P.barrier()
        T_KB = [76 * K, 92 * K]
        T_VB = [108 * K, 108 * K + 8320]
        T_KA = 92 * K
        T_VA = 108 * K
        T_CK = 126 * K
        T_P = 158 * K
        T_YA = 164 * K
        T_RC = 172 * K
        T_WKV = 180 * K
        KA = v16(T_KA, SEQ)
        VA = v16(T_VA, 64 * 130).rearrange("p (t g d) -> p t g d", t=64, g=2)
        CK = v16(T_CK, 2 * SEQ).rearrange("p (c t) -> p c t", c=2)
        Wkv = v16(T_WKV, 2 * 1024).rearrange("p (c n) -> p c n", c=2)
        ga = P.group("attA")
        for r in range(4):
            P.dma("sp", ga, KA[:, r * T:(r + 1) * T], kva[1][r * 160:r * 160 + 128, :],
                  reads=[("kvall", par, 1)], writes=["KA"])
            for g in range(2):
                vsrc = bass.AP(tensor=kva[2], offset=r * 128 * T + g * 64, ap=[[128, 128], [128 * 128, 16], [1, 64]])
                P.dma("sp", ga, VA[:, r * 16:(r + 1) * 16, g, 0:64], vsrc, reads=[("kvall", par, 2)], writes=["VA"])
        P.op("dve", "memset", writes=["VA1"], ap=VA[:, :, :, 64:65], constant=1.0)
        gc = P.group("attC")
        for r in range(4):
            P.dma("sp", gc, CK[:, :, r * T:(r + 1) * T],
                  kva[0].ap()[r * 256:(r + 1) * 256, :].rearrange("(c p) t -> p c t", p=128),
                  reads=[("kvall", par, 0)], writes=["CK"])
        gwk = P.group("w7")
        P.dma("pool", gwk, Wkv, w_kvup[l].rearrange("(c p) n -> p c n", p=128), writes=["Wkv"])

        Sb = [PS[0], PS[1]]
        pend = []
        state = {"g": 0, "acc": 0}

        def attention(jobs):
            units = []
            for jb in jobs:
                for u in range(jb["n"]):
                    units.append((jb, u))
            n = len(units)
            jobset = {}
            for ji, jb in enumerate(jobs):
                jobset[id(jb)] = (state["acc"] + ji) % 2
            state["acc"] += len(jobs)

            def accbank(jb, i):
                return 4 + 2 * jobset[id(jb)] + i

            def qk(ui):
                jb, u = units[ui]
                s_ = state["g"] + ui
                for t, (lh, rh) in enumerate(jb["qk"](u)):
                    mm(Sb[s_ % 2][:, t * 512:(t + 1) * 512], lh, rh, True, True,
                       reads=jb["rk"], writes=[("S", s_ % 2)])

            def ex(ui):
                jb, u = units[ui]
                s_ = state["g"] + ui
                pt = v16(T_P + (s_ % 3) * 2 * K, 1024)
                P.op("act", "activation", reads=[("S", s_ % 2)], writes=[("P", s_ % 3)], out=pt, in_=Sb[s_ % 2][:, :],
                     func=AF.Exp, scale=jb["scale"])

            def pv(ui):
                jb, u = units[ui]
                s_ = state["g"] + ui
                pt = v16(T_P + (s_ % 3) * 2 * K, 1024)
                for t, (ai, lh, st, sp) in enumerate(jb["pv"](u)):
                    bk = accbank(jb, ai)
                    mm(bank(bk, 0, 65), lh, pt[:, t * 512:(t + 1) * 512], st, sp,
                       reads=[("P", s_ % 3)] + jb["vk"], writes=[("ps", bk)])
                if u == jb["n"] - 1:
                    for ai in range(jb["nacc"]):
                        bk = accbank(jb, ai)
                        slot = bk - 4
                        ya_off = T_YA + slot * 2 * K
                        rc_off = T_RC + (slot % 2) * 4 * K
                        ya = v32(ya_off, 512, 0, 65)
                        rc = v32(rc_off, 512, 64, 65)
                        rch = v16(rc_off + 2 * K, 512, 64, 65)
                        rcl = v16(rc_off + 3 * K, 512, 64, 65)
                        P.op("dve", "tensor_copy", reads=[("ps", bk)], writes=[("ya", slot)], out=ya, in_=bank(bk, 0, 65))
                        P.op("dve", "reciprocal", reads=[("ya", slot)], writes=[("rc", slot % 2)], out=rc,
                             in_=v32(ya_off, 512, 64, 65))
                        P.op("dve", "tensor_copy", reads=[("rc", slot % 2)], writes=[("rch", slot % 2)], out=rch, in_=rc)
                        P.op("dve", "tensor_tensor", reads=[("rc", slot % 2), ("rch", slot % 2)], writes=[("rcl", slot % 2)],
                             out=rcl, in0=rc, in1=rch, op=ALU.subtract)
                        dst = jb["dst"][ai]
                        yk = jb["ykeys"][ai]

                        def stage2(bk=bk, slot=slot, ya_off=ya_off, rch=rch, rcl=rcl, dst=dst, yk=yk):
                            mm(bank(bk, 0, 64), ones_f[64:65, 0:64], rch, True, False, reads=[("rch", slot % 2), "ones"],
                               writes=[("ps", bk)])
                            mm(bank(bk, 0, 64), ones_f[64:65, 0:64], rcl, False, True, reads=[("rcl", slot % 2), "ones"],
                               writes=[("ps", bk)])
                            P.op("dve", "tensor_tensor", reads=[("ya", slot), ("ps", bk)], writes=[yk], out=dst,
                                 in0=v32(ya_off, 512, 0, 64), in1=bank(bk, 0, 64), op=ALU.mult)
                        pend.append([3 + ai, stage2])

            for ui in range(n + 2):
                if ui < n:
                    qk(ui)
                    ex(ui)
                if ui >= 2:
                    pv(ui - 2)
                for pp in list(pend):
                    pp[0] -= 1
                    if pp[0] <= 0:
                        pp[1]()
                        pend.remove(pp)
                if ui < n:
                    jb, u = units[ui]
                    exl = jb.get("extra", [])
                    if exl and u % 2 == 1 and u // 2 < len(exl):
                        exl[u // 2]()
            for pp in list(pend):
                pp[1]()
                pend.remove(pp)
            state["g"] += n

        jobs = []
        for pc in range(4):
            for qc in range(4):
                jobs.append(dict(
                    n=64, nacc=2, scale=1.0 / 8.0,
                    qk=lambda u, pc=pc, qc=qc: [(KA[g * 64:(g + 1) * 64, u * 128:(u + 1) * 128],
                                                 QA(g * 64, (g + 1) * 64, pc, qc * 512, (qc + 1) * 512)) for g in range(2)],
                    pv=lambda u: [(g, VA[:, u, g, :], u == 0, u == 63) for g in range(2)],
                    dst=[YA(g * 64, (g + 1) * 64, pc, qc * 512, (qc + 1) * 512) for g in range(2)],
                    ykeys=[("YA", g, pc, qc) for g in range(2)],
                    rk=["KA"] + [("QA", pc, tb) for tb in range(4)], vk=["VA", "VA1"]))
        attention(jobs)

        if stop == "GQA":
            break
        P.barrier()
        KBt = [v16(T_KB[i], SEQ, 0, 96) for i in range(2)]
        VBt = [v16(T_VB[i], 64 * 65).rearrange("p (t d) -> p t d", t=64) for i in range(2)]
        gkr = P.group("attK")
        for i in range(2):
            for r in range(4):
                P.dma("sp", gkr, v16(T_KB[i], SEQ, 64, 96)[:, r * T:(r + 1) * T], kva[1][r * 160 + 128:r * 160 + 160, :],
                      reads=[("kvall", par, 1)], writes=[("KBr", i)])
            P.op("dve", "memset", writes=[("VB1", i)], ap=VBt[i][:, :, 64:65], constant=1.0)

        def expansion_units(h):
            i = h % 2
            us = []
            for ch in range(16):
                def f(ch=ch):
                    b = 7
                    for c in range(2):
                        mm(bank(b, 0, 64), Wkv[:, c, h * 128:h * 128 + 64], CK[:, c, ch * 512:(ch + 1) * 512], c == 0, c == 1,
                           reads=["Wkv", "CK"], writes=[("ps", b)])
                    P.op("dve", "tensor_copy", reads=[("ps", b)], writes=[("KB", i)], out=v16(T_KB[i], SEQ, 0, 64)[:, ch * 512:(ch + 1) * 512],
                                                        in_=bank(b, 0, 64))
                us.append(f)
            for t8 in range(8):
                def f(t8=t8):
                    b = 7
                    for tt in range(8):
                        kt = t8 * 8 + tt
                        for c in range(2):
                            mm(bank(b)[:, tt * 64:(tt + 1) * 64], CK[:, c, kt * 128:(kt + 1) * 128],
                               Wkv[:, c, h * 128 + 64:h * 128 + 128], c == 0, c == 1,
                               reads=["Wkv", "CK"], writes=[("ps", b)])
                    P.op("dve", "tensor_copy", reads=[("ps", b)], writes=[("VB", i)], out=VBt[i][:, t8 * 8:(t8 + 1) * 8, 0:64],
                                                        in_=bank(b).rearrange("p (t d) -> p t d", t=8))
                us.append(f)
            return us

        for f in expansion_units(0):
            f()
        jobs = []
        for h in range(8):
            i = h % 2
            for qc in range(4):
                jobs.append(dict(
                    n=32, nacc=1, scale=1.0 / float(np.sqrt(96.0)),
                    qk=lambda u, i=i, h=h, qc=qc: [(KBt[i][:, (2 * u + t) * 128:(2 * u + t + 1) * 128],
                                                   QB(0, 96, h, qc * 512, (qc + 1) * 512)) for t in range(2)],
                    pv=lambda u, i=i: [(0, VBt[i][:, 2 * u + t, :], 2 * u + t == 0, 2 * u + t == 63) for t in range(2)],
                    dst=[YB((h % 2) * 64, (h % 2) * 64 + 64, h // 2, qc * 512, (qc + 1) * 512)],
                    ykeys=[("YB", h, qc)],
                    rk=[("KB", i), ("KBr", i)] + [("QB", h, tb, s_) for tb in range(4) for s_ in range(2)],
                    vk=[("VB", i), ("VB1", i)],
                    extra=(expansion_units(h + 1)[qc * 6:(qc + 1) * 6] if h < 7 else [])))
        attention(jobs)

        if stop == "MLA":
            break
        P.barrier()
        M_WG = 44 * K
        M_WAB = 76 * K
        M_WO = 92 * K
        M_UT = 108 * K
        M_XB = 124 * K
        M_T = 156 * K
        Wg = v16(M_WG, 8 * 2048).rearrange("p (k n) -> p k n", k=8)
        Wab = [v16(M_WAB + i * 8 * K, 4 * 1024).rearrange("p (k n) -> p k n", k=4) for i in range(2)]
        Wo = v16(M_WO, 8 * 1024).rearrange("p (k n) -> p k n", k=8)
        P.dma("pool", P.group("w0"), Wg[:, :, 0:256], w_g[l].rearrange("(k p) n -> p k n", p=128)[:, :, 0:256], writes=[("Wg", 0)])
        P.dma("pool", P.group("w1"), Wg[:, :, 1024:1280], w_g[l].rearrange("(k p) n -> p k n", p=128)[:, :, 1024:1280], writes=[("Wg", 1)])
        gm = P.group("w2")
        P.dma("pool", gm, Wab[0], w_a[l].rearrange("(k p) n -> p k n", p=128), writes=["Wab"])
        P.dma("pool", gm, Wab[1], w_b[l].rearrange("(k p) n -> p k n", p=128), writes=["Wab"])
        P.dma("pool", P.group("w3"), Wg[:, :, 256:1024], w_g[l].rearrange("(k p) n -> p k n", p=128)[:, :, 256:1024], writes=[("Wg", 2)])
        P.dma("pool", P.group("w4"), Wg[:, :, 1280:2048], w_g[l].rearrange("(k p) n -> p k n", p=128)[:, :, 1280:2048], writes=[("Wg", 3)])
        P.dma("pool", P.group("w5"), Wo, w_o[l].rearrange("(k p) n -> p k n", p=128), writes=["Wo"])
        m_ta = [v32(M_T + i * 2 * K, 512) for i in range(2)]
        m_tb = [v32(M_T + 4 * K + i * 2 * K, 512) for i in range(2)]
        m_u1 = [v32(M_T + 8 * K + i * 2 * K, 512) for i in range(2)]
        m_u2 = [v32(M_T + 12 * K + i * 2 * K, 512) for i in range(2)]
        m_mTs = [v16(M_T + 16 * K, 8 * 512).rearrange("p (c t) -> p c t", c=8),
                 v16(M_T + 36 * K, 8 * 512).rearrange("p (c t) -> p c t", c=8)]
        m_tmp = [v32(M_T + 24 * K + i * 4 * K, 1024) for i in range(2)]
        m_ss = v32(M_T + 32 * K, 8)
        m_mse = v32(M_T + 32 * K + 64, 8)
        m_rstd = v32(M_T + 32 * K + 128, 8)
        m_junk = v16(M_T + 33 * K, 1024)
        def m_load(tb):
            c0, c1 = tb * 512, (tb + 1) * 512
            xb = v32(M_XB + (tb % 2) * 16 * K, 4 * 1024).rearrange("p (j d) -> p j d", j=4)
            uT = v16(M_UT + (tb % 2) * 8 * K, 8 * 512).rearrange("p (k t) -> p k t", k=8)
            uk = ("muT", tb % 2)
            m_mT = m_mTs[tb % 2]
            gx = P.group("mx%d" % (tb % 2))
            P.dma("sp", gx, xb, x_src[c0:c1, :].rearrange("(j p) d -> p j d", p=128),
                  reads=[("xcur", tb)], writes=[("mxb", tb % 2, j) for j in range(4)])
            P.dma("sp", gx, uT, uT_scr.ap().rearrange("(k p) t -> p k t", p=128)[:, :, c0:c1],
                  reads=[("uscr", tb)], writes=[uk])

        def m_cl(tb):
            c0, c1 = tb * 512, (tb + 1) * 512
            xb = v32(M_XB + (tb % 2) * 16 * K, 4 * 1024).rearrange("p (j d) -> p j d", j=4)
            uT = v16(M_UT + (tb % 2) * 8 * K, 8 * 512).rearrange("p (k t) -> p k t", k=8)
            uk = ("muT", tb % 2)
            m_mT = m_mTs[tb % 2]
            for c in range(8):
                i = c % 2
                bg1, bg2, ba, bb = nbank(), nbank(), nbank(), nbank()
                for k in range(8):
                    mm(bank(bg1), Wg[:, k, c * 128:(c + 1) * 128], uT[:, k, :], k == 0, k == 7,
                       reads=[("Wg", 0 if c < 2 else 2), uk], writes=[("ps", bg1)])
                for k in range(8):
                    mm(bank(bg2), Wg[:, k, 1024 + c * 128:1024 + (c + 1) * 128], uT[:, k, :], k == 0, k == 7,
                       reads=[("Wg", 1 if c < 2 else 3), uk], writes=[("ps", bg2)])
                for k in range(4):
                    mm(bank(ba), Wab[0][:, k, c * 128:(c + 1) * 128], YA(0, 128, k, c0, c1), k == 0, k == 3,
                       reads=["Wab"] + [("YA", g, k, tb) for g in range(2)], writes=[("ps", ba)])
                for k in range(4):
                    mm(bank(bb), Wab[1][:, k, c * 128:(c + 1) * 128], YB(0, 128, k, c0, c1), k == 0, k == 3,
                       reads=["Wab"] + [("YB", 2 * k + s, tb) for s in range(2)], writes=[("ps", bb)])
                P.op("act", "activation", reads=[("ps", bg1), "bhalf"], writes=[("mta", i)], out=m_ta[i], in_=bank(bg1), func=AF.Tanh,
                                                                     bias=bhalf[:, c:c + 1], scale=0.5)
                P.op("act", "activation", reads=[("ps", bg2), "bhalf"], writes=[("mtb", i)], out=m_tb[i], in_=bank(bg2), func=AF.Tanh,
                                                                     bias=bhalf[:, 8 + c:9 + c], scale=0.5)
                P.op("dve", "scalar_tensor_tensor", reads=[("mta", i), ("ps", ba)], writes=[("mu1", i)], out=m_u1[i], in0=m_ta[i], scalar=1.0, in1=bank(ba),
                                                                        op0=ALU.add, op1=ALU.mult)
                P.op("dve", "scalar_tensor_tensor", reads=[("mtb", i), ("ps", bb)], writes=[("mu2", i)], out=m_u2[i], in0=m_tb[i], scalar=1.0, in1=bank(bb),
                                                                        op0=ALU.add, op1=ALU.mult)
                P.op("dve", "tensor_tensor", reads=[("mu1", i), ("mu2", i)], writes=[("mT", tb % 2, c)], out=m_mT[:, c, :], in0=m_u1[i], in1=m_u2[i], op=ALU.add)

        def m_op(tb):
            c0, c1 = tb * 512, (tb + 1) * 512
            xb = v32(M_XB + (tb % 2) * 16 * K, 4 * 1024).rearrange("p (j d) -> p j d", j=4)
            uT = v16(M_UT + (tb % 2) * 8 * K, 8 * 512).rearrange("p (k t) -> p k t", k=8)
            uk = ("muT", tb % 2)
            m_mT = m_mTs[tb % 2]
            for j in range(4):
                pr = npair()
                for hf in range(2):
                    for c in range(8):
                        mm(PS[pr][:, hf * 512:(hf + 1) * 512], m_mT[:, c, j * 128:(j + 1) * 128],
                           Wo[:, c, hf * 512:(hf + 1) * 512], c == 0, c == 7,
                           reads=["Wo", ("mT", tb % 2, c)], writes=[("ps", 2 * pr + hf)])
                P.op("act", "activation", reads=[("ps", 2 * pr), ("ps", 2 * pr + 1)], writes=[("mss", j), "mjunk"], out=m_junk, in_=PS[pr][:, :], func=AF.Square,
                                                              accum_out=m_ss[:, j:j + 1])
                P.op("dve", "tensor_scalar", reads=[("mss", j)], writes=[("mmse", j)], out=m_mse[:, j:j + 1], in0=m_ss[:, j:j + 1], scalar1=1.0 / D,
                                                           scalar2=4.0 * EPS, op0=ALU.mult, op1=ALU.add)
                P.op("pool", "tensor_tensor", reads=[("mmse", j), "expt"], writes=[("mrstd", j)], out=m_rstd[:, j:j + 1], in0=m_mse[:, j:j + 1], in1=expt[:, 0:1],
                                                            op=ALU.pow)
                tmp = m_tmp[j % 2]
                P.op("dve", "tensor_tensor", reads=[("ps", 2 * pr), ("ps", 2 * pr + 1), ("gpost", 0)], writes=[("mtmp", j % 2)], out=tmp, in0=PS[pr][:, :], in1=gpost[0], op=ALU.mult)
                P.op("dve", "scalar_tensor_tensor", reads=[("mtmp", j % 2), ("mrstd", j), ("mxb", tb % 2, j)], writes=[("mxb", tb % 2, j)], out=xb[:, j, :], in0=tmp,
                                                                                 scalar=m_rstd[:, j:j + 1], in1=xb[:, j, :],
                                                                                 op0=ALU.mult, op1=ALU.add)
            gs = P.group("mst%d" % (tb % 2))
            P.dma("sp", gs, x_cur[c0:c1, :].rearrange("(j p) d -> p j d", p=128), xb,
                  reads=[("mxb", tb % 2, j) for j in range(4)], writes=[("xcur", tb)])

        m_load(0)
        m_cl(0)
        for tb in range(NB):
            if tb + 1 < NB:
                m_load(tb + 1)
                m_cl(tb + 1)
            m_op(tb)

        if stop == "M":
            break
        P.barrier()
        F_WU = 12 * K
        F_WD = 76 * K
        F_AT = 140 * K
        F_XB = 172 * K
        F_UT = 188 * K
        F_T = 196 * K
        Wu = v16(F_WU, 8 * 4096).rearrange("p (k n) -> p k n", k=8)
        Wd = v16(F_WD, 32 * 1024).rearrange("p (k n) -> p k n", k=32)
        for q in range(4):
            P.dma("pool", P.group("w%d" % q), Wu[:, :, q * 1024:(q + 1) * 1024],
                  w_up[l].rearrange("(k p) n -> p k n", p=128)[:, :, q * 1024:(q + 1) * 1024], writes=[("Wu", q)])
        for q in range(4):
            P.dma("pool", P.group("w%d" % (4 + q)), Wd[:, q * 8:(q + 1) * 8, :],
                  w_dn[l].rearrange("(k p) n -> p k n", p=128)[:, q * 8:(q + 1) * 8, :], writes=[("Wd", q)])
        FB = 8
        aTs = [v16(F_AT + i * 16 * K, 32 * 256).rearrange("p (k t) -> p k t", k=32) for i in range(2)]
        f_ur = [v16(F_T + i * 2 * K, 1024) for i in range(2)]
        f_r = [v32(F_T + 4 * K + i * 1 * K, 256) for i in range(4)]
        f_ss = v32(F_T + 8 * K, 8)
        f_mse = v32(F_T + 8 * K + 64, 8)
        f_rstd = v32(F_T + 8 * K + 128, 8)
        f_ss2 = v32(F_T + 8 * K + 192, 8)
        f_mse2 = v32(F_T + 8 * K + 256, 8)
        f_rstd2 = v32(F_T + 8 * K + 320, 8)
        f_junk = v16(F_T + 9 * K, 1024 // 2)
        xbs = [v32(F_XB + i * 8 * K, 2 * 1024).rearrange("p (j d) -> p j d", j=2) for i in range(2)]
        u2s = [v16(F_UT + i * 4 * K, 8 * 256).rearrange("p (k t) -> p k t", k=8) for i in range(2)]

        def f_load(tb):
            i = tb % 2
            P.dma("sp", P.group("fx%d" % i), xbs[i], x_cur[tb * 256:(tb + 1) * 256, :].rearrange("(j p) d -> p j d", p=128),
                  reads=[("xcur", tb // 2)], writes=[("fxb", i, j) for j in range(2)])

        def f_norm(tb):
            i = tb % 2
            xb, u2 = xbs[i], u2s[i]
            for j in range(2):
                for hf in range(2):
                    P.op("act", "activation", reads=[("fxb", i, j)], writes=[("fss", i, j, hf), "fjunk"], out=f_junk,
                         in_=xb[:, j, hf * 512:(hf + 1) * 512], func=AF.Square,
                         accum_out=f_ss[:, hf * 4 + i * 2 + j:hf * 4 + i * 2 + j + 1])
            P.op("dve", "tensor_tensor", reads=[("fss", i, j, hf) for j in range(2) for hf in range(2)], writes=[("fmse", i)],
                 out=f_mse[:, i * 2:i * 2 + 2], in0=f_ss[:, i * 2:i * 2 + 2], in1=f_ss[:, 4 + i * 2:4 + i * 2 + 2], op=ALU.add)
            P.op("dve", "tensor_scalar", reads=[("fmse", i)], writes=[("fmse", i)], out=f_mse[:, i * 2:i * 2 + 2],
                 in0=f_mse[:, i * 2:i * 2 + 2], scalar1=1.0 / D, scalar2=EPS, op0=ALU.mult, op1=ALU.add)
            P.op("pool", "tensor_tensor", reads=[("fmse", i), "expt"], writes=[("frstd", i)], out=f_rstd[:, i * 2:i * 2 + 2],
                 in0=f_mse[:, i * 2:i * 2 + 2], in1=expt[:, 0:2], op=ALU.pow)
            for j in range(2):
                ur = f_ur[j]
                urk = ("fur", j)
                P.op("dve", "tensor_scalar", reads=[("fxb", i, j), ("frstd", i)], writes=[urk], out=ur, in0=xb[:, j, :],
                     scalar1=f_rstd[:, i * 2 + j:i * 2 + j + 1], scalar2=None, op0=ALU.mult)
                b = nbank()
                pT = bank(b).bitcast(BF16).rearrange("p (c t) -> p c t", t=128)
                for c in range(8):
                    P.op("pe", "transpose", reads=[urk, "ident"], writes=[("ps", b)], out=pT[:, c, :],
                         in_=ur[:, c * 128:(c + 1) * 128], identity=ident)
                for c in range(8):
                    P.op("dve", "tensor_scalar", reads=[("ps", b), "vec"], writes=[("u2", i, j, c)],
                         out=u2[:, c, j * 128:(j + 1) * 128], in0=pT[:, c, :], scalar1=vec[:, 8 + c:9 + c], scalar2=None,
                         op0=ALU.mult)

        def f_up(tb):
            i = tb % 2
            u2, aT = u2s[i], aTs[i]
            u2keys = [("u2", i, j, c) for j in range(2) for c in range(8)]
            for fc in range(32):
                b = nbank()
                for k in range(8):
                    mm(bank(b)[:, 0:256], Wu[:, k, fc * 128:(fc + 1) * 128], u2[:, k, :], k == 0, k == 7,
                       reads=(u2keys if k in (0, 7) else []) + [("Wu", fc // 8)], writes=[("ps", b)])
                r = f_r[fc % 4]
                P.op("act", "activation", reads=[("ps", b)], writes=[("fr", fc % 4)], out=r, in_=bank(b)[:, 0:256], func=AF.Relu)
                eng = "dve" if fc % 2 == 0 else "pool"
                P.op(eng, "tensor_tensor", reads=[("fr", fc % 4)], writes=[("aT", i, fc)], out=aT[:, fc, :], in0=r, in1=r,
                     op=ALU.mult)

        def f_down(tb):
            i = tb % 2
            xb, aT = xbs[i], aTs[i]
            for j in range(2):
                pr = npair()
                for hf in range(2):
                    for fc in range(32):
                        mm(PS[pr][:, hf * 512:(hf + 1) * 512], aT[:, fc, j * 128:(j + 1) * 128],
                           Wd[:, fc, hf * 512:(hf + 1) * 512], fc == 0, fc == 31,
                           reads=[("aT", i, fc), ("Wd", fc // 8)], writes=[("ps", 2 * pr + hf)])
                for hf in range(2):
                    P.op("act", "activation", reads=[("ps", 2 * pr + hf)], writes=[("fss2", j, hf), "fjunk"], out=f_junk,
                         in_=PS[pr][:, hf * 512:(hf + 1) * 512], func=AF.Square, accum_out=f_ss2[:, hf * 2 + j:hf * 2 + j + 1])
                P.op("dve", "tensor_tensor", reads=[("fss2", j, 0), ("fss2", j, 1)], writes=[("fmse2", j)],
                     out=f_mse2[:, j:j + 1], in0=f_ss2[:, j:j + 1], in1=f_ss2[:, 2 + j:3 + j], op=ALU.add)
                P.op("dve", "tensor_scalar", reads=[("fmse2", j)], writes=[("fmse2", j)], out=f_mse2[:, j:j + 1],
                     in0=f_mse2[:, j:j + 1], scalar1=1.0 / D, scalar2=EPS, op0=ALU.mult, op1=ALU.add)
                P.op("pool", "tensor_tensor", reads=[("fmse2", j), "expt"], writes=[("frstd2", j)], out=f_rstd2[:, j:j + 1],
                     in0=f_mse2[:, j:j + 1], in1=expt[:, 0:1], op=ALU.pow)
                tmp = v32(F_T + 4 * K, 1024)
                P.op("dve", "tensor_tensor", reads=[("ps", 2 * pr), ("ps", 2 * pr + 1), ("gpost", 1)],
                     writes=[("fr", q) for q in range(4)], out=tmp, in0=PS[pr][:, :], in1=gpost[1], op=ALU.mult)
                P.op("dve", "scalar_tensor_tensor", reads=[("fr", q) for q in range(4)] + [("frstd2", j), ("fxb", i, j)],
                     writes=[("fxb", i, j)], out=xb[:, j, :], in0=tmp, scalar=f_rstd2[:, j:j + 1], in1=xb[:, j, :],
                     op0=ALU.mult, op1=ALU.add)
            P.dma("sp", P.group("fst%d" % i), x_dst_final[tb * 256:(tb + 1) * 256, :].rearrange("(j p) d -> p j d", p=128), xb,
                  reads=[("fxb", i, j) for j in range(2)], writes=[("xcur", tb // 2), ("xout", tb)])

        f_load(0)
        f_norm(0)
        for tb in range(FB):
            if tb + 1 < FB:
                f_load(tb + 1)
            f_up(tb)
            if tb + 1 < FB:
                f_norm(tb + 1)
            f_down(tb)

    P.barrier()
    P.op("sp", None)
    P.lower(stack)
    stack.close()
    return nc


def _perm(d):
    h = d // 2
    q = h // 2
    idx = np.arange(d)
    within = idx % h
    base = idx - within
    src = np.where(within < q, within + q, within - q) + base
    sign = np.where(within < q, -1.0, 1.0).astype(np.float32)
    return src, sign


def _rope_tables(rot_dim):
    rows = SEQ // 64
    row = np.repeat(np.arange(rows, dtype=np.float32), 64)
    col = np.tile(np.arange(64, dtype=np.float32), rows)
    half = rot_dim // 2
    inv = (np.float32(10000.0) ** (-np.arange(0, half, 2, dtype=np.float32) / np.float32(half))).astype(np.float32)
    ar = row[:, None] * inv[None, :]
    ac = col[:, None] * inv[None, :]
    ang = np.concatenate([ar, ar, ac, ac], axis=-1).astype(np.float32)
    return np.cos(ang).astype(np.float32), np.sin(ang).astype(np.float32)


_CACHE = {}


def _prep(inputs):
    f = lambda a: np.ascontiguousarray(np.asarray(a, dtype=np.float32))
    w_in = f(inputs["w_in"])
    L = w_in.shape[0]
    pA, sA = _perm(64)
    pB, sB = _perm(32)
    qcols = np.concatenate([np.concatenate([pc * 64 + np.arange(64), (4 + pc) * 64 + np.arange(64)]) for pc in range(4)])
    qpcols = np.concatenate([np.concatenate([pc * 64 + pA, (4 + pc) * 64 + pA]) for pc in range(4)])
    kcols = 512 + np.arange(128)
    kpcols = 512 + np.concatenate([pA, 64 + pA])
    cols = np.concatenate([qcols, qpcols, kcols, kpcols, 768 + np.arange(384), 1152 + np.arange(256),
                           1408 + np.arange(32), 1408 + pB, 640 + np.arange(128)])
    assert cols.shape[0] == NCOLA
    w_inA = np.ascontiguousarray(w_in[:, :, cols])
    w_g = np.ascontiguousarray(w_in[:, :, 1440:3488])
    wq = f(inputs["w_q_up"])
    qc = []
    for h in range(8):
        qc.append(h * 96 + np.arange(96))
        qc.append(np.concatenate([h * 96 + np.arange(64), h * 96 + 64 + pB]))
    w_qup = np.ascontiguousarray(wq[:, :, np.concatenate(qc)])
    wa = f(inputs["w_branch_a"])
    arow = np.concatenate([np.concatenate([pc * 64 + np.arange(64), (4 + pc) * 64 + np.arange(64)]) for pc in range(4)])
    w_a = np.ascontiguousarray(wa[:, arow, :])
    vecP = np.zeros((L, 128, 48), np.float32)
    vecP[:, :, 0:8] = f(inputs["pre_mix_g"]).reshape(L, 8, 128).transpose(0, 2, 1)
    vecP[:, :, 8:16] = f(inputs["pre_ffn_g"]).reshape(L, 8, 128).transpose(0, 2, 1)
    vecP[:, :, 16:19] = f(inputs["q_a_norm_g"]).reshape(L, 3, 128).transpose(0, 2, 1)
    vecP[:, :, 19:21] = f(inputs["kv_a_norm_g"]).reshape(L, 2, 128).transpose(0, 2, 1)
    qg = f(inputs["q_norm_g"])
    kg = f(inputs["k_norm_g"])
    vecP[:, :, 21] = np.concatenate([qg, qg], axis=1)
    vecP[:, :, 22] = np.concatenate([qg[:, pA], qg[:, pA]], axis=1)
    vecP[:, :, 23] = np.concatenate([kg, kg], axis=1)
    vecP[:, :, 24] = np.concatenate([kg[:, pA], kg[:, pA]], axis=1)
    vecP[:, :, 25:41] = f(inputs["b_gate"]).reshape(L, 16, 128).transpose(0, 2, 1)
    vecB = np.ascontiguousarray(np.stack([f(inputs["post_mix_g"]), f(inputs["post_ffn_g"])], axis=1))
    cA, sAt = _rope_tables(64)
    cB, sBt = _rope_tables(32)
    tabA_full = np.stack([np.concatenate([cA.T, cA.T], 0), np.concatenate([(sAt * sA[None, :]).T] * 2, 0)], 0)
    tabB_full = np.stack([cB.T, (sBt * sB[None, :]).T], 0)
    cst = np.zeros((3, 128, 128), np.float32)
    cst[0] = 1.0
    cst[1, 0:64, 0:64] = 1.0
    cst[1, 64:128, 64:128] = 1.0
    cst = cst.astype(ml_dtypes.bfloat16)
    ident = np.eye(128, dtype=np.float32).astype(ml_dtypes.bfloat16)
    shared = dict(w_inA=w_inA, w_g=w_g, w_qup=w_qup, w_kvup=f(inputs["w_kv_up"]), w_a=w_a, w_b=f(inputs["w_branch_b"]),
                  w_o=f(inputs["w_o"]), w_up=f(inputs["w_ffn_up"]), w_dn=f(inputs["w_ffn_down"]), vecP=vecP, vecB=vecB,
                  cst=cst, identd=ident)
    tabs = []
    for r in range(4):
        tabs.append(dict(tabA=np.ascontiguousarray(tabA_full[:, :, r * T:(r + 1) * T].astype(np.float32)),
                         tabB=np.ascontiguousarray(tabB_full[:, :, r * T:(r + 1) * T].astype(np.float32))))
    return shared, tabs, L


STOP = None


def _get_prog(depth):
    if depth not in _CACHE:
        _CACHE[depth] = build_program(depth, STOP)
    return _CACHE[depth]


def kernel(**inputs):
    x = np.ascontiguousarray(np.asarray(inputs["x"], dtype=np.float32))
    shared, tabs, L = _prep(inputs)
    per_layer = ("w_inA", "w_g", "w_qup", "w_kvup", "w_a", "w_b", "w_o", "w_up", "w_dn", "vecP", "vecB")
    xs = [np.ascontiguousarray(x[c // 4, (c % 4) * T:(c % 4 + 1) * T, :]) for c in range(8)]
    if FUSED:
        nc = _get_prog(L)
        in_maps = []
        for c in range(8):
            m = dict(shared)
            m.update(tabs[c % 4])
            m["x_in"] = xs[c]
            in_maps.append(m)
        res = run_bass_kernel_spmd(nc, in_maps, core_ids=list(range(8)))
        xs = [np.asarray(res.results[c]["out"]) for c in range(8)]
    else:
        nc = _get_prog(1)
        for l in range(L):
            in_maps = []
            for c in range(8):
                m = {k: (np.ascontiguousarray(v[l:l + 1]) if k in per_layer else v) for k, v in shared.items()}
                m.update(tabs[c % 4])
                m["x_in"] = xs[c]
                in_maps.append(m)
            res = run_bass_kernel_spmd(nc, in_maps, core_ids=list(range(8)))
            xs = [np.ascontiguousarray(np.asarray(res.results[c]["out"])) for c in range(8)]
    outp = np.empty_like(x)
    for c in range(8):
        outp[c // 4, (c % 4) * T:(c % 4 + 1) * T, :] = xs[c]
    return outp
```

```python
import numpy as np
import ml_dtypes
import concourse.bass as bass
import concourse.mybir as mybir
from concourse.bass_utils import run_bass_kernel_spmd
from contextlib import ExitStack

F32 = mybir.dt.float32
BF16 = mybir.dt.bfloat16
ALU = mybir.AluOpType
AF = mybir.ActivationFunctionType

D = 1024
SEQ = 8192
T = 2048
NB = 4
EPS = 1e-6
NCOLA = 2112
KVROWS = 544
FUSED = True


class Op:
    __slots__ = ("eng", "name", "kw", "deps", "signal", "sigval", "grp", "kind", "ep")

    def __init__(self, eng, name, kw, kind="c"):
        self.ep = 0
        self.eng = eng
        self.name = name
        self.kw = kw
        self.deps = []
        self.signal = False
        self.sigval = 0
        self.grp = None
        self.kind = kind


class Grp:
    __slots__ = ("sem", "final")

    def __init__(self, sem):
        self.sem = sem
        self.final = 0


class Prog:
    ENGS = ("pe", "act", "dve", "pool", "sp")

    def __init__(self, nc):
        self.nc = nc
        self.streams = {e: [] for e in self.ENGS}
        self.lastw = {}
        self.readers = {}
        self.bar = []
        self.dcount = {}
        self.dlast = {}
        self.cccount = 0
        self.epoch = 0

    def _deps(self, o, reads, writes):
        deps = []
        for k in reads:
            w = self.lastw.get(k)
            if w is not None:
                deps.append(w)
        for k in writes:
            w = self.lastw.get(k)
            if w is not None:
                deps.append(w)
            deps.extend(self.readers.get(k, ()))
        deps.extend(self.bar)
        o.deps = deps
        for k in reads:
            self.readers.setdefault(k, []).append(o)
        for k in writes:
            self.lastw[k] = o
            self.readers[k] = []

    def op(self, eng, name, reads=(), writes=(), **kw):
        o = Op(eng, name, kw)
        o.ep = self.epoch
        self._deps(o, reads, writes)
        self.streams[eng].append(o)
        return o

    def group(self, sem):
        g = Grp(sem)
        return g

    def dma(self, queue, g, out, in_, reads=(), writes=()):
        o = Op(queue, "dma_start", dict(out=out, in_=in_), kind="d")
        o.grp = g
        self._deps(o, reads, writes)
        prev = self.dlast.get(g.sem)
        if prev is not None and prev.grp is not g:
            o.deps.append(prev)
        self.dcount[g.sem] = self.dcount.get(g.sem, 0) + 16
        g.final = self.dcount[g.sem]
        self.dlast[g.sem] = o
        self.streams[queue].append(o)
        return o

    def collective(self, args, reads=(), writes=(), **kw):
        o = Op("pool", "collective_compute", kw, kind="cc")
        o.grp = args
        self._deps(o, reads, writes)
        self.cccount += 1
        o.sigval = self.cccount
        self.streams["pool"].append(o)
        return o

    def barrier(self):
        b = []
        for e in self.ENGS:
            for o in reversed(self.streams[e]):
                if o.kind == "c" or o.kind == "cc":
                    b.append(o)
                    break
        b.extend(self.dlast.values())
        self.bar = b

    def lower(self, stack):
        nc = self.nc
        for e in self.ENGS:
            for o in self.streams[e]:
                for d in o.deps:
                    if d.kind == "c" and not (d.eng == "pe" and o.eng == "pe"):
                        d.signal = True
        sems = {}
        for e in ("pe", "act", "dve", "pool"):
            cnts = {}
            for o in self.streams[e]:
                if o.kind == "c" and o.signal:
                    cnts[o.ep] = cnts.get(o.ep, 0) + 1
                    o.sigval = cnts[o.ep]
                    if (e, o.ep) not in sems:
                        sems[(e, o.ep)] = stack.enter_context(nc.semaphore("s_%s%d" % (e, o.ep)))
        sems["cc"] = stack.enter_context(nc.semaphore("s_cc"))
        dsems = {}
        for name in self.dcount:
            dsems[name] = stack.enter_context(nc.semaphore("d_" + name))
        block = stack.enter_context(nc.Block())

        def run(ename, eng):
            waited = {}
            for o in self.streams[ename]:
                need = {}
                for d in o.deps:
                    if d.kind == "d":
                        if o.kind == "d" and d.grp is o.grp:
                            continue
                        key, val = ("d", d.grp.sem), d.grp.final
                    elif d.kind == "cc":
                        key, val = ("cc",), d.sigval
                    else:
                        if d.eng == "pe" and ename == "pe":
                            continue
                        key, val = ("c", d.eng, d.ep), d.sigval
                    if need.get(key, 0) < val:
                        need[key] = val
                for key, val in need.items():
                    if waited.get(key, 0) >= val:
                        continue
                    waited[key] = val
                    if key[0] == "d":
                        eng.wait_ge(dsems[key[1]], val)
                    elif key[0] == "cc":
                        eng.wait_ge(sems["cc"], val)
                    else:
                        eng.wait_ge(sems[(key[1], key[2])], val)
                if o.name is None:
                    continue
                if o.kind == "cc":
                    ins = eng.collective_compute(*o.grp, **o.kw)
                else:
                    ins = getattr(eng, o.name)(**o.kw)
                if o.kind == "d":
                    ins.then_inc(dsems[o.grp.sem], 16)
                elif o.kind == "cc":
                    ins.then_inc(sems["cc"], 1)
                elif o.signal:
                    ins.then_inc(sems[(o.eng, o.ep)], 1)

        @block.tensor
        def _(eng):
            run("pe", eng)

        @block.scalar
        def _(eng):
            run("act", eng)

        @block.vector
        def _(eng):
            run("dve", eng)

        @block.gpsimd
        def _(eng):
            run("pool", eng)

        @block.sync
        def _(eng):
            run("sp", eng)


def build_program(depth, stop=None):
    nc = bass.Bass("TRN2", target_bir_lowering=False)
    P = Prog(nc)

    def din(name, shape, dt=F32):
        return nc.dram_tensor(name, shape, dt, kind="ExternalInput")

    x_in = din("x_in", [T, D])
    w_inA = din("w_inA", [depth, D, NCOLA])
    w_g = din("w_g", [depth, D, 2048])
    w_qup = din("w_qup", [depth, 384, 1536])
    w_kvup = din("w_kvup", [depth, 256, 1024])
    w_a = din("w_a", [depth, 512, 1024])
    w_b = din("w_b", [depth, 512, 1024])
    w_o = din("w_o", [depth, D, D])
    w_up = din("w_up", [depth, D, 4096])
    w_dn = din("w_dn", [depth, 4096, D])
    vecP = din("vecP", [depth, 128, 48])
    vecB = din("vecB", [depth, 2, D])
    tabA = din("tabA", [2, 128, T])
    tabB = din("tabB", [2, 32, T])
    cst = din("cst", [3, 128, 128], BF16)
    identd = din("identd", [128, 128], BF16)
    out = nc.dram_tensor("out", [T, D], F32, kind="ExternalOutput")

    x_cur = nc.dram_tensor("x_cur", [T, D], F32)
    uT_scr = nc.dram_tensor("uT_scr", [D, T], BF16)
    KVR = (256, 160, 128)
    kvf_own = [[nc.dram_tensor("kvf_own%d_%d" % (i, q), [KVR[q], T], BF16) for q in range(3)] for i in range(2)]
    kvf_all = [[nc.dram_tensor("kvf_all%d_%d" % (i, q), [4 * KVR[q], T], BF16) for q in range(3)] for i in range(2)]

    stack = ExitStack()
    NBF = 105472
    SB = stack.enter_context(nc.sbuf_tensor("sb", [128, NBF], BF16))
    PS = [stack.enter_context(nc.psum_tensor("ps%d" % i, [128, 1024], F32)) for i in range(4)]

    def v16(off, n, p0=0, p1=128):
        assert off % 4 == 0 and off + 2 * n <= NBF * 2, (off, n)
        return SB[p0:p1, off // 2: off // 2 + n]

    def v32(off, n, p0=0, p1=128):
        assert off % 4 == 0 and off + 4 * n <= NBF * 2, (off, n)
        return SB[p0:p1, off // 2: off // 2 + 2 * n].bitcast(F32)

    def bank(b, p0=0, p1=128):
        return PS[b // 2][p0:p1, (b % 2) * 512:(b % 2 + 1) * 512]

    K = 1024
    C_ONES = 0
    C_BLK = 512
    C_ID = 1024
    C_EXP = 1280
    C_VEC = 3328
    C_BH = 3520
    C_GB = 3584
    C_END = 12 * K
    ones_f = v16(C_ONES, 128)
    blk_f = v16(C_BLK, 128)
    ident = v16(C_ID, 128)
    expt = v32(C_EXP, 512)
    vec = v32(C_VEC, 48)
    bhalf = v32(C_BH, 16)
    gpost = [v32(C_GB + i * 4096, 1024) for i in range(2)]

    g0 = P.group("const")
    P.dma("sp", g0, ones_f, cst[0, :, :], writes=["ones"])
    P.dma("sp", g0, blk_f, cst[1, :, :], writes=["blk"])
    P.dma("sp", g0, ident, identd[:, :], writes=["ident"])
    P.op("dve", "memset", writes=["expt"], ap=expt, constant=-0.5)
    epsb = v32(C_GB + 8192, 16)
    P.op("dve", "memset", writes=["epsb"], ap=epsb, constant=EPS)

    O_Y = 12 * K
    O_QB = 44 * K
    O_QA = 76 * K

    def YA(p0, p1, pc, c0, c1):
        return SB[p0:p1, (O_Y // 2 + pc * T + c0):(O_Y // 2 + pc * T + c1)]

    def YB(p0, p1, pc, c0, c1):
        return SB[p0:p1, (O_Y // 2 + 4 * T + pc * T + c0):(O_Y // 2 + 4 * T + pc * T + c1)]

    def QB(p0, p1, h, c0, c1):
        return SB[p0:p1, (O_QB // 2 + h * T + c0):(O_QB // 2 + h * T + c1)]

    def QA(p0, p1, pc, c0, c1):
        return SB[p0:p1, (O_QA // 2 + pc * T + c0):(O_QA // 2 + pc * T + c1)]

    psrr = [0]

    def nbank():
        b = psrr[0] % 8
        psrr[0] += 1
        return b

    pprr = [0]

    def npair():
        p = pprr[0] % 4
        pprr[0] += 1
        return p

    def mm(outap, lhsT, rhs, start, stop, reads, writes):
        return P.op("pe", "matmul", reads=reads, writes=writes, out=outap, lhsT=lhsT, rhs=rhs, start=start, stop=stop)

    def rms_tokens(xb_f, xkey, nt, ss, mse, rstd, junk, eps, key):
        for j in range(nt):
            P.op("act", "activation", reads=[xkey], writes=[(key, "ss", j), (key, "junk")], out=junk, in_=xb_f[:, j, :], func=AF.Square,
                                                    accum_out=ss[:, j:j + 1])
        P.op("dve", "tensor_scalar", reads=[(key, "ss", j) for j in range(nt)], writes=[(key, "mse")], out=mse[:, 0:nt], in0=ss[:, 0:nt], scalar1=1.0 / D, scalar2=eps,
                                              op0=ALU.mult, op1=ALU.add)
        P.op("pool", "tensor_tensor", reads=[(key, "mse"), "expt"], writes=[(key, "rstd")], out=rstd[:, 0:nt], in0=mse[:, 0:nt], in1=expt[:, 0:nt], op=ALU.pow)

    for l in range(depth):
        P.epoch = l
        par = l % 2
        x_src = x_in if l == 0 else x_cur
        x_dst_final = out if (l == depth - 1) else x_cur
        P.barrier()
        A_T = 12 * K
        A_WA = 92 * K
        A_WQ = 126 * K
        A_XB = 136 * K
        A_UT = 168 * K
        A_TB = 184 * K
        A_UR = 192 * K
        A_ST = 196 * K
        Wa = v16(A_WA, 8 * NCOLA).rearrange("p (k n) -> p k n", k=8)
        Wq = v16(A_WQ, 3 * 1536).rearrange("p (k n) -> p k n", k=3)
        for q, (ca, cb) in enumerate(((1024, NCOLA), (0, 1024))):
            P.dma("pool", P.group("w%d" % q), Wa[:, :, ca:cb], w_inA[l].rearrange("(k p) n -> p k n", p=128)[:, :, ca:cb],
                  writes=[("Wa", q)])
        P.dma("pool", P.group("w2"), Wq, w_qup[l].rearrange("(k p) n -> p k n", p=128), writes=["Wq"])
        gv = P.group("vec")
        P.dma("sp", gv, vec, vecP[l, :, :], writes=["vec"])
        for i in range(2):
            P.dma("sp", gv, gpost[i], bass.AP(tensor=vecB, offset=(l * 2 + i) * D, ap=[[0, 128], [1, D]]),
                  writes=[("gpost", i)])
        P.op("dve", "tensor_scalar", reads=["vec"], writes=["bhalf"], out=bhalf, in0=vec[:, 25:41], scalar1=0.5, scalar2=None, op0=ALU.mult)
        t_ss = v32(A_T, 8)
        t_mse = v32(A_T + 64, 8)
        t_rstd = v32(A_T + 128, 8)
        t_junk = v16(A_T + 256, 1024)
        t_sq = [v16(A_T + 4 * K + i * 2 * K, 512) for i in range(3)]
        t_ms = [v32(A_T + 10 * K + i * 2 * K, 512) for i in range(2)]
        t_rs = [v32(A_T + 14 * K + i * 2 * K, 512) for i in range(2)]
        t_1 = [v32(A_T + 18 * K + i * 2 * K, 512) for i in range(2)]
        t_2 = [v32(A_T + 22 * K + i * 2 * K, 512) for i in range(2)]
        t_cqn = v16(A_T + 26 * K, 3 * 512).rearrange("p (c t) -> p c t", c=3)
        cnt = {"sq": 0, "ms": 0, "t": 0}
        cosA = v32(A_TB, 512)
        sinA = v32(A_TB + 2 * K, 512)
        cosB = lambda p0, p1: v32(A_TB + 4 * K, 512, p0, p1)
        sinB = lambda p0, p1: v32(A_TB + 6 * K, 512, p0, p1)
        st_k = v16(A_ST, 512)
        st_ckv = v16(A_ST + 1 * K, 2 * 512).rearrange("p (c t) -> p c t", c=2)
        st_kr = v16(A_ST + 3 * K, 512, 0, 32)
        st_v = v16(A_ST + 4 * K, 4 * 128).rearrange("p (j c) -> p j c", j=4)
        kvo = kvf_own[par]
        kva = kvf_all[par]

        def a_bufs(tb):
            xb = v32(A_XB + (tb % 2) * 16 * K, 4 * 1024).rearrange("p (j d) -> p j d", j=4)
            uT = v16(A_UT + (tb % 2) * 8 * K, 8 * 512).rearrange("p (k t) -> p k t", k=8)
            uk = ("uT", tb % 2)
            ukeys = [(uk, j, c) for j in range(4) for c in range(8)]
            return xb, ("xb", tb % 2), uT, uk, ukeys

        def a_load(tb):
            xb, xk, uT, uk, ukeys = a_bufs(tb)
            P.dma("sp", P.group("xb%d" % (tb % 2)), xb, x_src[tb * 512:(tb + 1) * 512, :].rearrange("(j p) d -> p j d", p=128),
                  reads=[("xcur", tb)], writes=[xk])

        def a_tables(tb):
            c0, c1 = tb * 512, (tb + 1) * 512
            gt = P.group("tab")
            P.dma("sp", gt, cosA, tabA[0, :, c0:c1], writes=["tab"])
            P.dma("sp", gt, sinA, tabA[1, :, c0:c1], writes=["tab"])
            for (p0, p1) in ((0, 32), (64, 96)):
                P.dma("sp", gt, cosB(p0, p1), tabB[0, :, c0:c1], writes=["tab"])
                P.dma("sp", gt, sinB(p0, p1), tabB[1, :, c0:c1], writes=["tab"])

        def a_norm(tb):
            c0, c1 = tb * 512, (tb + 1) * 512
            xb, xk, uT, uk, ukeys = a_bufs(tb)
            rms_tokens(xb, xk, 4, t_ss, t_mse, t_rstd, t_junk, EPS, "nA")
            for j in range(4):
                ur = v16(A_UR + (j % 2) * 2 * K, 1024)
                urk = ("ur", j % 2)
                P.op("dve", "tensor_scalar", reads=[xk, ("nA", "rstd")], writes=[urk], out=ur, in0=xb[:, j, :],
                     scalar1=t_rstd[:, j:j + 1], scalar2=None, op0=ALU.mult)
                b = nbank()
                pT = bank(b).bitcast(BF16).rearrange("p (c t) -> p c t", t=128)
                for c in range(8):
                    P.op("pe", "transpose", reads=[urk, "ident"], writes=[("ps", b)], out=pT[:, c, :],
                         in_=ur[:, c * 128:(c + 1) * 128], identity=ident)
                for c in range(8):
                    P.op("dve", "tensor_scalar", reads=[("ps", b), "vec"], writes=[(uk, j, c)],
                         out=uT[:, c, j * 128:(j + 1) * 128], in0=pT[:, c, :], scalar1=vec[:, c:c + 1], scalar2=None,
                         op0=ALU.mult)
            P.dma("sp", P.group("ust%d" % (tb % 2)), uT_scr.ap().rearrange("(k p) t -> p k t", p=128)[:, :, c0:c1], uT,
                  reads=ukeys, writes=[("uscr", tb)])

        def a_helpers(tb):
            xb, xk, uT, uk, ukeys = a_bufs(tb)
            def proj(col0, M, b, p0=0):
                for k in range(8):
                    wk = ("Wa", 0 if col0 >= 1024 else 1)
                    mm(bank(b, p0, p0 + M), Wa[:, k, col0:col0 + M], uT[:, k, :], k == 0, k == 7,
                       reads=(ukeys if k in (0, 7) else []) + [wk], writes=[("ps", b)])

            def stat(sqs, blkones, scale_eps, nparts=128):
                b = nbank()
                n = len(sqs)
                for i, (sq, sk) in enumerate(sqs):
                    mm(bank(b), blkones, sq, i == 0, i == n - 1, reads=[sk, "ones", "blk"], writes=[("ps", b)])
                mi = cnt["ms"] % 2
                cnt["ms"] += 1
                ms, rs = t_ms[mi], t_rs[mi]
                P.op("act", "activation", reads=[("ps", b)], writes=[("ms", mi)], out=ms, in_=bank(b), func=AF.Sqrt,
                     scale=scale_eps[0], bias=epsb[:, 0:1])
                P.op("dve", "reciprocal", reads=[("ms", mi)], writes=[("rs", mi)], out=rs, in_=ms)
                return rs, ("rs", mi)

            def square_of(b):
                si = cnt["sq"] % 3
                cnt["sq"] += 1
                sq = t_sq[si]
                P.op("act", "activation", reads=[("ps", b)], writes=[("sq", si)], out=sq, in_=bank(b), func=AF.Square)
                return sq, ("sq", si)

            def rope_combine(b1, b2, gcol, gpcol, cs, sn, outap, outkey, rs=None, rsk=None, p0=0, p1=128):
                ti = cnt["t"] % 2
                cnt["t"] += 1
                a1 = v32(A_T + 18 * K + ti * 2 * K, 512, p0, p1)
                a2 = v32(A_T + 22 * K + ti * 2 * K, 512, p0, p1)
                if gcol is None:
                    P.op("dve", "tensor_tensor", reads=[("ps", b1), "tab"], writes=[("t1", ti)], out=a1, in0=bank(b1, p0, p1), in1=cs, op=ALU.mult)
                    P.op("dve", "tensor_tensor", reads=[("ps", b2), "tab"], writes=[("t2", ti)], out=a2, in0=bank(b2, p0, p1), in1=sn, op=ALU.mult)
                else:
                    P.op("dve", "scalar_tensor_tensor", reads=[("ps", b1), "tab", "vec"], writes=[("t1", ti)], out=a1, in0=bank(b1, p0, p1), scalar=vec[p0:p1, gcol:gcol + 1],
                                                                 in1=cs, op0=ALU.mult, op1=ALU.mult)
                    P.op("dve", "scalar_tensor_tensor", reads=[("ps", b2), "tab", "vec"], writes=[("t2", ti)], out=a2, in0=bank(b2, p0, p1), scalar=vec[p0:p1, gpcol:gpcol + 1],
                                                                 in1=sn, op0=ALU.mult, op1=ALU.mult)
                if rs is None:
                    P.op("pool", "tensor_tensor", reads=[("t1", ti), ("t2", ti)], writes=[outkey], out=outap, in0=a1, in1=a2, op=ALU.add)
                else:
                    P.op("pool", "tensor_tensor", reads=[("t1", ti), ("t2", ti)], writes=[("t1", ti)], out=a1, in0=a1, in1=a2, op=ALU.add)
                    P.op("pool", "tensor_tensor", reads=[("t1", ti), rsk], writes=[outkey], out=outap, in0=a1, in1=rs, op=ALU.mult)

            return proj, stat, square_of, rope_combine

        def a_kv(tb):
            c0, c1 = tb * 512, (tb + 1) * 512
            xb, xk, uT, uk, ukeys = a_bufs(tb)
            proj, stat, square_of, rope_combine = a_helpers(tb)
            gk = P.group("kvst")
            b1, b2 = nbank(), nbank()
            proj(1024, 128, b1)
            proj(1152, 128, b2)
            sq, sk = square_of(b1)
            rs, rsk = stat([(sq, sk)], blk_f, (1.0 / 64, EPS))
            rope_combine(b1, b2, 23, 24, cosA, sinA, st_k, ("st", "k"), rs, rsk)
            P.dma("sp", gk, kvo[1][0:128, c0:c1], st_k, reads=[("st", "k")], writes=[("kvown", par)])
            bs = [nbank() for _ in range(2)]
            sqs = []
            for c in range(2):
                proj(1664 + c * 128, 128, bs[c])
                sqs.append(square_of(bs[c]))
            rs, rsk = stat(sqs, ones_f, (1.0 / 256, EPS))
            for c in range(2):
                P.op("dve", "scalar_tensor_tensor", reads=[("ps", bs[c]), rsk, "vec"], writes=[("st", "ckv", c)], out=st_ckv[:, c, :], in0=bank(bs[c]),
                                                                       scalar=vec[:, 19 + c:20 + c], in1=rs,
                                                                       op0=ALU.mult, op1=ALU.mult)
            P.dma("sp", gk, kvo[0].ap()[0:256, c0:c1].rearrange("(c p) t -> p c t", p=128), st_ckv,
                  reads=[("st", "ckv", 0), ("st", "ckv", 1)], writes=[("kvown", par)])
            b1, b2 = nbank(), nbank()
            proj(1920, 32, b1)
            proj(1952, 32, b2)
            rope_combine(b1, b2, None, None, cosB(0, 32), sinB(0, 32), st_kr, ("st", "kr"), p0=0, p1=32)
            P.dma("sp", gk, kvo[1][128:160, c0:c1], st_kr, reads=[("st", "kr")], writes=[("kvown", par)])
            b = nbank()
            for j in range(4):
                for k in range(8):
                    mm(bank(b)[:, j * 128:(j + 1) * 128], uT[:, k, j * 128:(j + 1) * 128], Wa[:, k, 1984:2112],
                       k == 0, k == 7, reads=(ukeys if k in (0, 7) else []) + [("Wa", 0)], writes=[("ps", b)])
            P.op("act", "copy", reads=[("ps", b)], writes=[("st", "v")], out=st_v, in_=bank(b).rearrange("p (j c) -> p j c", j=4))
            vdst = bass.AP(tensor=kvo[2], offset=c0 * 128, ap=[[128, 128], [128 * 128, 4], [1, 128]])
            P.dma("sp", gk, vdst, st_v, reads=[("st", "v")], writes=[("kvown", par)])


        def a_q(tb):
            c0, c1 = tb * 512, (tb + 1) * 512
            xb, xk, uT, uk, ukeys = a_bufs(tb)
            proj, stat, square_of, rope_combine = a_helpers(tb)
            for pc in range(4):
                b1, b2 = nbank(), nbank()
                proj(pc * 128, 128, b1)
                proj(512 + pc * 128, 128, b2)
                sq, sk = square_of(b1)
                rs, rsk = stat([(sq, sk)], blk_f, (1.0 / 64, EPS))
                rope_combine(b1, b2, 21, 22, cosA, sinA, QA(0, 128, pc, c0, c1), ("QA", pc, tb), rs, rsk)
            bs = [nbank() for _ in range(3)]
            sqs = []
            for c in range(3):
                proj(1280 + c * 128, 128, bs[c])
                sqs.append(square_of(bs[c]))
            rs, rsk = stat(sqs, ones_f, (1.0 / 384, EPS))
            for c in range(3):
                P.op("dve", "scalar_tensor_tensor", reads=[("ps", bs[c]), rsk, "vec"], writes=[("cqn", c)], out=t_cqn[:, c, :], in0=bank(bs[c]),
                                                                       scalar=vec[:, 16 + c:17 + c], in1=rs,
                                                                       op0=ALU.mult, op1=ALU.mult)
            for h in range(8):
                b1, b2 = nbank(), nbank()
                for c in range(3):
                    mm(bank(b1, 0, 96), Wq[:, c, h * 192:h * 192 + 96], t_cqn[:, c, :], c == 0, c == 2,
                       reads=[("cqn", c), "Wq"], writes=[("ps", b1)])
                for c in range(3):
                    mm(bank(b2, 0, 96), Wq[:, c, h * 192 + 96:h * 192 + 192], t_cqn[:, c, :], c == 0, c == 2,
                       reads=[("cqn", c), "Wq"], writes=[("ps", b2)])
                P.op("act", "copy", reads=[("ps", b1)], writes=[("QB", h, tb, 0)], out=QB(0, 64, h, c0, c1), in_=bank(b1, 0, 64))
                rope_combine(b1, b2, None, None, cosB(64, 96), sinB(64, 96), QB(64, 96, h, c0, c1), ("QB", h, tb, 1),
                             p0=64, p1=96)

        def a_reload(tb):
            xb, xk, uT, uk, ukeys = a_bufs(tb)
            P.dma("sp", P.group("xb%d" % (tb % 2)), uT,
                  uT_scr.ap().rearrange("(k p) t -> p k t", p=128)[:, :, tb * 512:(tb + 1) * 512],
                  reads=[("uscr", tb)], writes=ukeys)

        if stop != "A0":
            a_load(0)
            a_norm(0)
            for tb in range(NB):
                if tb + 1 < NB:
                    a_load(tb + 1)
                a_tables(tb)
                a_kv(tb)
                if tb + 1 < NB:
                    a_norm(tb + 1)
            if stop not in ("A", "A1"):
                for q in range(3):
                    P.collective(("AllGather", ALU.bypass), reads=[("kvown", par)], writes=[("kvall", par, q)],
                                 replica_groups=[[0, 1, 2, 3], [4, 5, 6, 7]],
                                 ins=[kvf_own[par][q].ap().opt()], outs=[kva[q].ap().opt()])
            a_reload(0)
            for tb in range(NB):
                if tb + 1 < NB:
                    a_reload(tb + 1)
                a_tables(tb)
                a_q(tb)

        if stop in ("A", "A0", "A1"):
            break
        if stop == "AG":
            break
        P.barrier()
        T_KB = [76 * K, 92 * K]
        T_VB = [108 * K, 108 * K + 8320]
        T_KA = 92 * K
        T_VA = 108 * K
        T_CK = 126 * K
        T_P = 158 * K
        T_YA = 164 * K
        T_RC = 172 * K
        T_WKV = 180 * K
        KA = v16(T_KA, SEQ)
        VA = v16(T_VA, 64 * 130).rearrange("p (t g d) -> p t g d", t=64, g=2)
        CK = v16(T_CK, 2 * SEQ).rearrange("p (c t) -> p c t", c=2)
        Wkv = v16(T_WKV, 2 * 1024).rearrange("p (c n) -> p c n", c=2)
        ga = P.group("attA")
        for r in range(4):
            P.dma("sp", ga, KA[:, r * T:(r + 1) * T], kva[1][r * 160:r * 160 + 128, :],
                  reads=[("kvall", par, 1)], writes=["KA"])
            for g in range(2):
                vsrc = bass.AP(tensor=kva[2], offset=r * 128 * T + g * 64, ap=[[128, 128], [128 * 128, 16], [1, 64]])
                P.dma("sp", ga, VA[:, r * 16:(r + 1) * 16, g, 0:64], vsrc, reads=[("kvall", par, 2)], writes=["VA"])
        P.op("dve", "memset", writes=["VA1"], ap=VA[:, :, :, 64:65], constant=1.0)
        gc = P.group("attC")
        for r in range(4):
            P.dma("sp", gc, CK[:, :, r * T:(r + 1) * T],
                  kva[0].ap()[r * 256:(r + 1) * 256, :].rearrange("(c p) t -> p c t", p=128),
                  reads=[("kvall", par, 0)], writes=["CK"])
        gwk = P.group("w7")
        P.dma("pool", gwk, Wkv, w_kvup[l].rearrange("(c p) n -> p c n", p=128), writes=["Wkv"])

        Sb = [PS[0], PS[1]]
        pend = []
        state = {"g": 0, "acc": 0}

        def attention(jobs):
            units = []
            for jb in jobs:
                for u in range(jb["n"]):
                    units.append((jb, u))
            n = len(units)
            jobset = {}
            for ji, jb in enumerate(jobs):
                jobset[id(jb)] = (state["acc"] + ji) % 2
            state["acc"] += len(jobs)

            def accbank(jb, i):
                return 4 + 2 * jobset[id(jb)] + i

            def qk(ui):
                jb, u = units[ui]
                s_ = state["g"] + ui
                for t, (lh, rh) in enumerate(jb["qk"](u)):
                    mm(Sb[s_ % 2][:, t * 512:(t + 1) * 512], lh, rh, True, True,
                       reads=jb["rk"], writes=[("S", s_ % 2)])

            def ex(ui):
                jb, u = units[ui]
                s_ = state["g"] + ui
                pt = v16(T_P + (s_ % 3) * 2 * K, 1024)
                P.op("act", "activation", reads=[("S", s_ % 2)], writes=[("P", s_ % 3)], out=pt, in_=Sb[s_ % 2][:, :],
                     func=AF.Exp, scale=jb["scale"])

            def pv(ui):
                jb, u = units[ui]
                s_ = state["g"] + ui
                pt = v16(T_P + (s_ % 3) * 2 * K, 1024)
                for t, (ai, lh, st, sp) in enumerate(jb["pv"](u)):
                    bk = accbank(jb, ai)
                    mm(bank(bk, 0, 65), lh, pt[:, t * 512:(t + 1) * 512], st, sp,
                       reads=[("P", s_ % 3)] + jb["vk"], writes=[("ps", bk)])
                if u == jb["n"] - 1:
                    for ai in range(jb["nacc"]):
                        bk = accbank(jb, ai)
                        slot = bk - 4
                        ya_off = T_YA + slot * 2 * K
                        rc_off = T_RC + (slot % 2) * 4 * K
                        ya = v32(ya_off, 512, 0, 65)
                        rc = v32(rc_off, 512, 64, 65)
                        rch = v16(rc_off + 2 * K, 512, 64, 65)
                        rcl = v16(rc_off + 3 * K, 512, 64, 65)
                        P.op("dve", "tensor_copy", reads=[("ps", bk)], writes=[("ya", slot)], out=ya, in_=bank(bk, 0, 65))
                        P.op("dve", "reciprocal", reads=[("ya", slot)], writes=[("rc", slot % 2)], out=rc,
                             in_=v32(ya_off, 512, 64, 65))
                        P.op("dve", "tensor_copy", reads=[("rc", slot % 2)], writes=[("rch", slot % 2)], out=rch, in_=rc)
                        P.op("dve", "tensor_tensor", reads=[("rc", slot % 2), ("rch", slot % 2)], writes=[("rcl", slot % 2)],
                             out=rcl, in0=rc, in1=rch, op=ALU.subtract)
                        dst = jb["dst"][ai]
                        yk = jb["ykeys"][ai]

                        def stage2(bk=bk, slot=slot, ya_off=ya_off, rch=rch, rcl=rcl, dst=dst, yk=yk):
                            mm(bank(bk, 0, 64), ones_f[64:65, 0:64], rch, True, False, reads=[("rch", slot % 2), "ones"],
                               writes=[("ps", bk)])
                            mm(bank(bk, 0, 64), ones_f[64:65, 0:64], rcl, False, True, reads=[("rcl", slot % 2), "ones"],
                               writes=[("ps", bk)])
                            P.op("dve", "tensor_tensor", reads=[("ya", slot), ("ps", bk)], writes=[yk], out=dst,
                                 in0=v32(ya_off, 512, 0, 64), in1=bank(bk, 0, 64), op=ALU.mult)
                        pend.append([8 + 6 * ai, stage2])

            for ui in range(n + 2):
                if ui < n:
                    qk(ui)
                    ex(ui)
                if ui >= 2:
                    pv(ui - 2)
                for pp in list(pend):
                    pp[0] -= 1
                    if pp[0] <= 0:
                        pp[1]()
                        pend.remove(pp)
                if ui < n:
                    jb, u = units[ui]
                    exl = jb.get("extra", [])
                    if exl and u % 2 == 1 and u // 2 < len(exl):
                        exl[u // 2]()
            for pp in list(pend):
                pp[1]()
                pend.remove(pp)
            state["g"] += n

        jobs = []
        for pc in range(4):
            for qc in range(4):
                jobs.append(dict(
                    n=64, nacc=2, scale=1.0 / 8.0,
                    qk=lambda u, pc=pc, qc=qc: [(KA[g * 64:(g + 1) * 64, u * 128:(u + 1) * 128],
                                                 QA(g * 64, (g + 1) * 64, pc, qc * 512, (qc + 1) * 512)) for g in range(2)],
                    pv=lambda u: [(g, VA[:, u, g, :], u == 0, u == 63) for g in range(2)],
                    dst=[YA(g * 64, (g + 1) * 64, pc, qc * 512, (qc + 1) * 512) for g in range(2)],
                    ykeys=[("YA", g, pc, qc) for g in range(2)],
                    rk=["KA"] + [("QA", pc, tb) for tb in range(4)], vk=["VA", "VA1"]))
        attention(jobs)

        if stop == "GQA":
            break
        P.barrier()
        KBt = [v16(T_KB[i], SEQ, 0, 96) for i in range(2)]
        VBt = [v16(T_VB[i], 64 * 65).rearrange("p (t d) -> p t d", t=64) for i in range(2)]
        gkr = P.group("attK")
        for i in range(2):
            for r in range(4):
                P.dma("sp", gkr, v16(T_KB[i], SEQ, 64, 96)[:, r * T:(r + 1) * T], kva[1][r * 160 + 128:r * 160 + 160, :],
                      reads=[("kvall", par, 1)], writes=[("KBr", i)])
            P.op("dve", "memset", writes=[("VB1", i)], ap=VBt[i][:, :, 64:65], constant=1.0)

        def expansion_units(h):
            i = h % 2
            us = []
            for ch in range(16):
                def f(ch=ch):
                    b = 7
                    for c in range(2):
                        mm(bank(b, 0, 64), Wkv[:, c, h * 128:h * 128 + 64], CK[:, c, ch * 512:(ch + 1) * 512], c == 0, c == 1,
                           reads=["Wkv", "CK"], writes=[("ps", b)])
                    P.op("dve", "tensor_copy", reads=[("ps", b)], writes=[("KB", i)], out=v16(T_KB[i], SEQ, 0, 64)[:, ch * 512:(ch + 1) * 512],
                                                        in_=bank(b, 0, 64))
                us.append(f)
            for t8 in range(8):
                def f(t8=t8):
                    b = 7
                    for tt in range(8):
                        kt = t8 * 8 + tt
                        for c in range(2):
                            mm(bank(b)[:, tt * 64:(tt + 1) * 64], CK[:, c, kt * 128:(kt + 1) * 128],
                               Wkv[:, c, h * 128 + 64:h * 128 + 128], c == 0, c == 1,
                               reads=["Wkv", "CK"], writes=[("ps", b)])
                    P.op("dve", "tensor_copy", reads=[("ps", b)], writes=[("VB", i)], out=VBt[i][:, t8 * 8:(t8 + 1) * 8, 0:64],
                                                        in_=bank(b).rearrange("p (t d) -> p t d", t=8))
                us.append(f)
            return us

        for f in expansion_units(0):
            f()
        jobs = []
        for h in range(8):
            i = h % 2
            for qc in range(4):
                jobs.append(dict(
                    n=32, nacc=1, scale=1.0 / float(np.sqrt(96.0)),
                    qk=lambda u, i=i, h=h, qc=qc: [(KBt[i][:, (2 * u + t) * 128:(2 * u + t + 1) * 128],
                                                   QB(0, 96, h, qc * 512, (qc + 1) * 512)) for t in range(2)],
                    pv=lambda u, i=i: [(0, VBt[i][:, 2 * u + t, :], 2 * u + t == 0, 2 * u + t == 63) for t in range(2)],
                    dst=[YB((h % 2) * 64, (h % 2) * 64 + 64, h // 2, qc * 512, (qc + 1) * 512)],
                    ykeys=[("YB", h, qc)],
                    rk=[("KB", i), ("KBr", i)] + [("QB", h, tb, s_) for tb in range(4) for s_ in range(2)],
                    vk=[("VB", i), ("VB1", i)],
                    extra=(expansion_units(h + 1)[qc * 6:(qc + 1) * 6] if h < 7 else [])))
        attention(jobs)

        if stop == "MLA":
            break
        P.barrier()
        M_WG = 44 * K
        M_WAB = 76 * K
        M_WO = 92 * K
        M_UT = 108 * K
        M_XB = 124 * K
        M_T = 156 * K
        Wg = v16(M_WG, 8 * 2048).rearrange("p (k n) -> p k n", k=8)
        Wab = [v16(M_WAB + i * 8 * K, 4 * 1024).rearrange("p (k n) -> p k n", k=4) for i in range(2)]
        Wo = v16(M_WO, 8 * 1024).rearrange("p (k n) -> p k n", k=8)
        P.dma("pool", P.group("w0"), Wg[:, :, 0:256], w_g[l].rearrange("(k p) n -> p k n", p=128)[:, :, 0:256], writes=[("Wg", 0)])
        P.dma("pool", P.group("w1"), Wg[:, :, 1024:1280], w_g[l].rearrange("(k p) n -> p k n", p=128)[:, :, 1024:1280], writes=[("Wg", 1)])
        gm = P.group("w2")
        P.dma("pool", gm, Wab[0], w_a[l].rearrange("(k p) n -> p k n", p=128), writes=["Wab"])
        P.dma("pool", gm, Wab[1], w_b[l].rearrange("(k p) n -> p k n", p=128), writes=["Wab"])
        P.dma("pool", P.group("w3"), Wg[:, :, 256:1024], w_g[l].rearrange("(k p) n -> p k n", p=128)[:, :, 256:1024], writes=[("Wg", 2)])
        P.dma("pool", P.group("w4"), Wg[:, :, 1280:2048], w_g[l].rearrange("(k p) n -> p k n", p=128)[:, :, 1280:2048], writes=[("Wg", 3)])
        P.dma("pool", P.group("w5"), Wo, w_o[l].rearrange("(k p) n -> p k n", p=128), writes=["Wo"])
        m_ta = [v32(M_T + i * 2 * K, 512) for i in range(2)]
        m_tb = [v32(M_T + 4 * K + i * 2 * K, 512) for i in range(2)]
        m_u1 = [v32(M_T + 8 * K + i * 2 * K, 512) for i in range(2)]
        m_u2 = [v32(M_T + 12 * K + i * 2 * K, 512) for i in range(2)]
        m_mTs = [v16(M_T + 16 * K, 8 * 512).rearrange("p (c t) -> p c t", c=8),
                 v16(M_T + 36 * K, 8 * 512).rearrange("p (c t) -> p c t", c=8)]
        m_tmp = [v32(M_T + 24 * K + i * 4 * K, 1024) for i in range(2)]
        m_ss = v32(M_T + 32 * K, 8)
        m_mse = v32(M_T + 32 * K + 64, 8)
        m_rstd = v32(M_T + 32 * K + 128, 8)
        m_junk = v16(M_T + 33 * K, 1024)
        def m_load(tb):
            c0, c1 = tb * 512, (tb + 1) * 512
            xb = v32(M_XB + (tb % 2) * 16 * K, 4 * 1024).rearrange("p (j d) -> p j d", j=4)
            uT = v16(M_UT + (tb % 2) * 8 * K, 8 * 512).rearrange("p (k t) -> p k t", k=8)
            uk = ("muT", tb % 2)
            m_mT = m_mTs[tb % 2]
            gx = P.group("mx%d" % (tb % 2))
            P.dma("sp", gx, xb, x_src[c0:c1, :].rearrange("(j p) d -> p j d", p=128),
                  reads=[("xcur", tb)], writes=[("mxb", tb % 2, j) for j in range(4)])
            P.dma("sp", gx, uT, uT_scr.ap().rearrange("(k p) t -> p k t", p=128)[:, :, c0:c1],
                  reads=[("uscr", tb)], writes=[uk])

        def m_cl(tb):
            c0, c1 = tb * 512, (tb + 1) * 512
            xb = v32(M_XB + (tb % 2) * 16 * K, 4 * 1024).rearrange("p (j d) -> p j d", j=4)
            uT = v16(M_UT + (tb % 2) * 8 * K, 8 * 512).rearrange("p (k t) -> p k t", k=8)
            uk = ("muT", tb % 2)
            m_mT = m_mTs[tb % 2]
            for c in range(8):
                i = c % 2
                bg1, bg2, ba, bb = nbank(), nbank(), nbank(), nbank()
                for k in range(8):
                    mm(bank(bg1), Wg[:, k, c * 128:(c + 1) * 128], uT[:, k, :], k == 0, k == 7,
                       reads=[("Wg", 0 if c < 2 else 2), uk], writes=[("ps", bg1)])
                for k in range(8):
                    mm(bank(bg2), Wg[:, k, 1024 + c * 128:1024 + (c + 1) * 128], uT[:, k, :], k == 0, k == 7,
                       reads=[("Wg", 1 if c < 2 else 3), uk], writes=[("ps", bg2)])
                for k in range(4):
                    mm(bank(ba), Wab[0][:, k, c * 128:(c + 1) * 128], YA(0, 128, k, c0, c1), k == 0, k == 3,
                       reads=["Wab"] + [("YA", g, k, tb) for g in range(2)], writes=[("ps", ba)])
                for k in range(4):
                    mm(bank(bb), Wab[1][:, k, c * 128:(c + 1) * 128], YB(0, 128, k, c0, c1), k == 0, k == 3,
                       reads=["Wab"] + [("YB", 2 * k + s, tb) for s in range(2)], writes=[("ps", bb)])
                P.op("act", "activation", reads=[("ps", bg1), "bhalf"], writes=[("mta", i)], out=m_ta[i], in_=bank(bg1), func=AF.Tanh,
                                                                     bias=bhalf[:, c:c + 1], scale=0.5)
                P.op("act", "activation", reads=[("ps", bg2), "bhalf"], writes=[("mtb", i)], out=m_tb[i], in_=bank(bg2), func=AF.Tanh,
                                                                     bias=bhalf[:, 8 + c:9 + c], scale=0.5)
                P.op("dve", "scalar_tensor_tensor", reads=[("mta", i), ("ps", ba)], writes=[("mu1", i)], out=m_u1[i], in0=m_ta[i], scalar=1.0, in1=bank(ba),
                                                                        op0=ALU.add, op1=ALU.mult)
                P.op("dve", "scalar_tensor_tensor", reads=[("mtb", i), ("ps", bb)], writes=[("mu2", i)], out=m_u2[i], in0=m_tb[i], scalar=1.0, in1=bank(bb),
                                                                        op0=ALU.add, op1=ALU.mult)
                P.op("dve", "tensor_tensor", reads=[("mu1", i), ("mu2", i)], writes=[("mT", tb % 2, c)], out=m_mT[:, c, :], in0=m_u1[i], in1=m_u2[i], op=ALU.add)

        def m_op(tb):
            c0, c1 = tb * 512, (tb + 1) * 512
            xb = v32(M_XB + (tb % 2) * 16 * K, 4 * 1024).rearrange("p (j d) -> p j d", j=4)
            uT = v16(M_UT + (tb % 2) * 8 * K, 8 * 512).rearrange("p (k t) -> p k t", k=8)
            uk = ("muT", tb % 2)
            m_mT = m_mTs[tb % 2]
            for j in range(4):
                pr = npair()
                for hf in range(2):
                    for c in range(8):
                        mm(PS[pr][:, hf * 512:(hf + 1) * 512], m_mT[:, c, j * 128:(j + 1) * 128],
                           Wo[:, c, hf * 512:(hf + 1) * 512], c == 0, c == 7,
                           reads=["Wo", ("mT", tb % 2, c)], writes=[("ps", 2 * pr + hf)])
                P.op("act", "activation", reads=[("ps", 2 * pr), ("ps", 2 * pr + 1)], writes=[("mss", j), "mjunk"], out=m_junk, in_=PS[pr][:, :], func=AF.Square,
                                                              accum_out=m_ss[:, j:j + 1])
                P.op("dve", "tensor_scalar", reads=[("mss", j)], writes=[("mmse", j)], out=m_mse[:, j:j + 1], in0=m_ss[:, j:j + 1], scalar1=1.0 / D,
                                                           scalar2=4.0 * EPS, op0=ALU.mult, op1=ALU.add)
                P.op("pool", "tensor_tensor", reads=[("mmse", j), "expt"], writes=[("mrstd", j)], out=m_rstd[:, j:j + 1], in0=m_mse[:, j:j + 1], in1=expt[:, 0:1],
                                                            op=ALU.pow)
                tmp = m_tmp[j % 2]
                P.op("dve", "tensor_tensor", reads=[("ps", 2 * pr), ("ps", 2 * pr + 1), ("gpost", 0)], writes=[("mtmp", j % 2)], out=tmp, in0=PS[pr][:, :], in1=gpost[0], op=ALU.mult)
                P.op("dve", "scalar_tensor_tensor", reads=[("mtmp", j % 2), ("mrstd", j), ("mxb", tb % 2, j)], writes=[("mxb", tb % 2, j)], out=xb[:, j, :], in0=tmp,
                                                                                 scalar=m_rstd[:, j:j + 1], in1=xb[:, j, :],
                                                                                 op0=ALU.mult, op1=ALU.add)
            gs = P.group("mst%d" % (tb % 2))
            P.dma("sp", gs, x_cur[c0:c1, :].rearrange("(j p) d -> p j d", p=128), xb,
                  reads=[("mxb", tb % 2, j) for j in range(4)], writes=[("xcur", tb)])

        m_load(0)
        m_cl(0)
        for tb in range(NB):
            if tb + 1 < NB:
                m_load(tb + 1)
                m_cl(tb + 1)
            m_op(tb)

        if stop == "M":
            break
        P.barrier()
        F_WU = 12 * K
        F_WD = 76 * K
        F_AT = 140 * K
        F_XB = 172 * K
        F_UT = 188 * K
        F_T = 196 * K
        Wu = v16(F_WU, 8 * 4096).rearrange("p (k n) -> p k n", k=8)
        Wd = v16(F_WD, 32 * 1024).rearrange("p (k n) -> p k n", k=32)
        for q in range(4):
            P.dma("pool", P.group("w%d" % q), Wu[:, :, q * 1024:(q + 1) * 1024],
                  w_up[l].rearrange("(k p) n -> p k n", p=128)[:, :, q * 1024:(q + 1) * 1024], writes=[("Wu", q)])
        for q in range(4):
            P.dma("pool", P.group("w%d" % (4 + q)), Wd[:, q * 8:(q + 1) * 8, :],
                  w_dn[l].rearrange("(k p) n -> p k n", p=128)[:, q * 8:(q + 1) * 8, :], writes=[("Wd", q)])
        FB = 8
        aTs = [v16(F_AT + i * 16 * K, 32 * 256).rearrange("p (k t) -> p k t", k=32) for i in range(2)]
        f_ur = [v16(F_T + i * 2 * K, 1024) for i in range(2)]
        f_r = [v32(F_T + 4 * K + i * 1 * K, 256) for i in range(4)]
        f_ss = v32(F_T + 8 * K, 8)
        f_mse = v32(F_T + 8 * K + 64, 8)
        f_rstd = v32(F_T + 8 * K + 128, 8)
        f_ss2 = v32(F_T + 8 * K + 192, 8)
        f_mse2 = v32(F_T + 8 * K + 256, 8)
        f_rstd2 = v32(F_T + 8 * K + 320, 8)
        f_junk = v16(F_T + 9 * K, 1024 // 2)
        xbs = [v32(F_XB + i * 8 * K, 2 * 1024).rearrange("p (j d) -> p j d", j=2) for i in range(2)]
        u2s = [v16(F_UT + i * 4 * K, 8 * 256).rearrange("p (k t) -> p k t", k=8) for i in range(2)]

        def f_load(tb):
            i = tb % 2
            P.dma("sp", P.group("fx%d" % i), xbs[i], x_cur[tb * 256:(tb + 1) * 256, :].rearrange("(j p) d -> p j d", p=128),
                  reads=[("xcur", tb // 2)], writes=[("fxb", i, j) for j in range(2)])

        def f_norm(tb):
            i = tb % 2
            xb, u2 = xbs[i], u2s[i]
            for j in range(2):
                for hf in range(2):
                    P.op("act", "activation", reads=[("fxb", i, j)], writes=[("fss", i, j, hf), "fjunk"], out=f_junk,
                         in_=xb[:, j, hf * 512:(hf + 1) * 512], func=AF.Square,
                         accum_out=f_ss[:, hf * 4 + i * 2 + j:hf * 4 + i * 2 + j + 1])
            P.op("dve", "tensor_tensor", reads=[("fss", i, j, hf) for j in range(2) for hf in range(2)], writes=[("fmse", i)],
                 out=f_mse[:, i * 2:i * 2 + 2], in0=f_ss[:, i * 2:i * 2 + 2], in1=f_ss[:, 4 + i * 2:4 + i * 2 + 2], op=ALU.add)
            P.op("dve", "tensor_scalar", reads=[("fmse", i)], writes=[("fmse", i)], out=f_mse[:, i * 2:i * 2 + 2],
                 in0=f_mse[:, i * 2:i * 2 + 2], scalar1=1.0 / D, scalar2=EPS, op0=ALU.mult, op1=ALU.add)
            P.op("pool", "tensor_tensor", reads=[("fmse", i), "expt"], writes=[("frstd", i)], out=f_rstd[:, i * 2:i * 2 + 2],
                 in0=f_mse[:, i * 2:i * 2 + 2], in1=expt[:, 0:2], op=ALU.pow)
            for j in range(2):
                ur = f_ur[j]
                urk = ("fur", j)
                P.op("dve", "tensor_scalar", reads=[("fxb", i, j), ("frstd", i)], writes=[urk], out=ur, in0=xb[:, j, :],
                     scalar1=f_rstd[:, i * 2 + j:i * 2 + j + 1], scalar2=None, op0=ALU.mult)
                b = nbank()
                pT = bank(b).bitcast(BF16).rearrange("p (c t) -> p c t", t=128)
                for c in range(8):
                    P.op("pe", "transpose", reads=[urk, "ident"], writes=[("ps", b)], out=pT[:, c, :],
                         in_=ur[:, c * 128:(c + 1) * 128], identity=ident)
                for c in range(8):
                    P.op("dve", "tensor_scalar", reads=[("ps", b), "vec"], writes=[("u2", i, j, c)],
                         out=u2[:, c, j * 128:(j + 1) * 128], in0=pT[:, c, :], scalar1=vec[:, 8 + c:9 + c], scalar2=None,
                         op0=ALU.mult)

        def f_up(tb):
            i = tb % 2
            u2, aT = u2s[i], aTs[i]
            u2keys = [("u2", i, j, c) for j in range(2) for c in range(8)]
            for fc in range(32):
                b = nbank()
                for k in range(8):
                    mm(bank(b)[:, 0:256], Wu[:, k, fc * 128:(fc + 1) * 128], u2[:, k, :], k == 0, k == 7,
                       reads=(u2keys if k in (0, 7) else []) + [("Wu", fc // 8)], writes=[("ps", b)])
                r = f_r[fc % 4]
                P.op("act", "activation", reads=[("ps", b)], writes=[("fr", fc % 4)], out=r, in_=bank(b)[:, 0:256], func=AF.Relu)
                eng = "dve" if fc % 2 == 0 else "pool"
                P.op(eng, "tensor_tensor", reads=[("fr", fc % 4)], writes=[("aT", i, fc)], out=aT[:, fc, :], in0=r, in1=r,
                     op=ALU.mult)

        def f_down(tb):
            i = tb % 2
            xb, aT = xbs[i], aTs[i]
            for j in range(2):
                pr = npair()
                for hf in range(2):
                    for fc in range(32):
                        mm(PS[pr][:, hf * 512:(hf + 1) * 512], aT[:, fc, j * 128:(j + 1) * 128],
                           Wd[:, fc, hf * 512:(hf + 1) * 512], fc == 0, fc == 31,
                           reads=[("aT", i, fc), ("Wd", fc // 8)], writes=[("ps", 2 * pr + hf)])
                for hf in range(2):
                    P.op("act", "activation", reads=[("ps", 2 * pr + hf)], writes=[("fss2", j, hf), "fjunk"], out=f_junk,
                         in_=PS[pr][:, hf * 512:(hf + 1) * 512], func=AF.Square, accum_out=f_ss2[:, hf * 2 + j:hf * 2 + j + 1])
                P.op("dve", "tensor_tensor", reads=[("fss2", j, 0), ("fss2", j, 1)], writes=[("fmse2", j)],
                     out=f_mse2[:, j:j + 1], in0=f_ss2[:, j:j + 1], in1=f_ss2[:, 2 + j:3 + j], op=ALU.add)
                P.op("dve", "tensor_scalar", reads=[("fmse2", j)], writes=[("fmse2", j)], out=f_mse2[:, j:j + 1],
                     in0=f_mse2[:, j:j + 1], scalar1=1.0 / D, scalar2=EPS, op0=ALU.mult, op1=ALU.add)
                P.op("pool", "tensor_tensor", reads=[("fmse2", j), "expt"], writes=[("frstd2", j)], out=f_rstd2[:, j:j + 1],
                     in0=f_mse2[:, j:j + 1], in1=expt[:, 0:1], op=ALU.pow)
                tmp = v32(F_T + 4 * K, 1024)
                P.op("dve", "tensor_tensor", reads=[("ps", 2 * pr), ("ps", 2 * pr + 1), ("gpost", 1)],
                     writes=[("fr", q) for q in range(4)], out=tmp, in0=PS[pr][:, :], in1=gpost[1], op=ALU.mult)
                P.op("dve", "scalar_tensor_tensor", reads=[("fr", q) for q in range(4)] + [("frstd2", j), ("fxb", i, j)],
                     writes=[("fxb", i, j)], out=xb[:, j, :], in0=tmp, scalar=f_rstd2[:, j:j + 1], in1=xb[:, j, :],
                     op0=ALU.mult, op1=ALU.add)
            P.dma("sp", P.group("fst%d" % i), x_dst_final[tb * 256:(tb + 1) * 256, :].rearrange("(j p) d -> p j d", p=128), xb,
                  reads=[("fxb", i, j) for j in range(2)], writes=[("xcur", tb // 2), ("xout", tb)])

        f_load(0)
        f_norm(0)
        for tb in range(FB):
            if tb + 1 < FB:
                f_load(tb + 1)
            f_up(tb)
            if tb + 1 < FB:
                f_norm(tb + 1)
            f_down(tb)

    P.barrier()
    P.op("sp", None)
    P.lower(stack)
    stack.close()
    return nc


def _perm(d):
    h = d // 2
    q = h // 2
    idx = np.arange(d)
    within = idx % h
    base = idx - within
    src = np.where(within < q, within + q, within - q) + base
    sign = np.where(within < q, -1.0, 1.0).astype(np.float32)
    return src, sign


def _rope_tables(rot_dim):
    rows = SEQ // 64
    row = np.repeat(np.arange(rows, dtype=np.float32), 64)
    col = np.tile(np.arange(64, dtype=np.float32), rows)
    half = rot_dim // 2
    inv = (np.float32(10000.0) ** (-np.arange(0, half, 2, dtype=np.float32) / np.float32(half))).astype(np.float32)
    ar = row[:, None] * inv[None, :]
    ac = col[:, None] * inv[None, :]
    ang = np.concatenate([ar, ar, ac, ac], axis=-1).astype(np.float32)
    return np.cos(ang).astype(np.float32), np.sin(ang).astype(np.float32)


_CACHE = {}


def _prep(inputs):
    f = lambda a: np.ascontiguousarray(np.asarray(a, dtype=np.float32))
    w_in = f(inputs["w_in"])
    L = w_in.shape[0]
    pA, sA = _perm(64)
    pB, sB = _perm(32)
    qcols = np.concatenate([np.concatenate([pc * 64 + np.arange(64), (4 + pc) * 64 + np.arange(64)]) for pc in range(4)])
    qpcols = np.concatenate([np.concatenate([pc * 64 + pA, (4 + pc) * 64 + pA]) for pc in range(4)])
    kcols = 512 + np.arange(128)
    kpcols = 512 + np.concatenate([pA, 64 + pA])
    cols = np.concatenate([qcols, qpcols, kcols, kpcols, 768 + np.arange(384), 1152 + np.arange(256),
                           1408 + np.arange(32), 1408 + pB, 640 + np.arange(128)])
    assert cols.shape[0] == NCOLA
    w_inA = np.ascontiguousarray(w_in[:, :, cols])
    w_g = np.ascontiguousarray(w_in[:, :, 1440:3488])
    wq = f(inputs["w_q_up"])
    qc = []
    for h in range(8):
        qc.append(h * 96 + np.arange(96))
        qc.append(np.concatenate([h * 96 + np.arange(64), h * 96 + 64 + pB]))
    w_qup = np.ascontiguousarray(wq[:, :, np.concatenate(qc)])
    wa = f(inputs["w_branch_a"])
    arow = np.concatenate([np.concatenate([pc * 64 + np.arange(64), (4 + pc) * 64 + np.arange(64)]) for pc in range(4)])
    w_a = np.ascontiguousarray(wa[:, arow, :])
    vecP = np.zeros((L, 128, 48), np.float32)
    vecP[:, :, 0:8] = f(inputs["pre_mix_g"]).reshape(L, 8, 128).transpose(0, 2, 1)
    vecP[:, :, 8:16] = f(inputs["pre_ffn_g"]).reshape(L, 8, 128).transpose(0, 2, 1)
    vecP[:, :, 16:19] = f(inputs["q_a_norm_g"]).reshape(L, 3, 128).transpose(0, 2, 1)
    vecP[:, :, 19:21] = f(inputs["kv_a_norm_g"]).reshape(L, 2, 128).transpose(0, 2, 1)
    qg = f(inputs["q_norm_g"])
    kg = f(inputs["k_norm_g"])
    vecP[:, :, 21] = np.concatenate([qg, qg], axis=1)
    vecP[:, :, 22] = np.concatenate([qg[:, pA], qg[:, pA]], axis=1)
    vecP[:, :, 23] = np.concatenate([kg, kg], axis=1)
    vecP[:, :, 24] = np.concatenate([kg[:, pA], kg[:, pA]], axis=1)
    vecP[:, :, 25:41] = f(inputs["b_gate"]).reshape(L, 16, 128).transpose(0, 2, 1)
    vecB = np.ascontiguousarray(np.stack([f(inputs["post_mix_g"]), f(inputs["post_ffn_g"])], axis=1))
    cA, sAt = _rope_tables(64)
    cB, sBt = _rope_tables(32)
    tabA_full = np.stack([np.concatenate([cA.T, cA.T], 0), np.concatenate([(sAt * sA[None, :]).T] * 2, 0)], 0)
    tabB_full = np.stack([cB.T, (sBt * sB[None, :]).T], 0)
    cst = np.zeros((3, 128, 128), np.float32)
    cst[0] = 1.0
    cst[1, 0:64, 0:64] = 1.0
    cst[1, 64:128, 64:128] = 1.0
    cst = cst.astype(ml_dtypes.bfloat16)
    ident = np.eye(128, dtype=np.float32).astype(ml_dtypes.bfloat16)
    shared = dict(w_inA=w_inA, w_g=w_g, w_qup=w_qup, w_kvup=f(inputs["w_kv_up"]), w_a=w_a, w_b=f(inputs["w_branch_b"]),
                  w_o=f(inputs["w_o"]), w_up=f(inputs["w_ffn_up"]), w_dn=f(inputs["w_ffn_down"]), vecP=vecP, vecB=vecB,
                  cst=cst, identd=ident)
    tabs = []
    for r in range(4):
        tabs.append(dict(tabA=np.ascontiguousarray(tabA_full[:, :, r * T:(r + 1) * T].astype(np.float32)),
                         tabB=np.ascontiguousarray(tabB_full[:, :, r * T:(r + 1) * T].astype(np.float32))))
    return shared, tabs, L


STOP = None


def _get_prog(depth):
    if depth not in _CACHE:
        _CACHE[depth] = build_program(depth, STOP)
    return _CACHE[depth]


def kernel(**inputs):
    x = np.ascontiguousarray(np.asarray(inputs["x"], dtype=np.float32))
    shared, tabs, L = _prep(inputs)
    per_layer = ("w_inA", "w_g", "w_qup", "w_kvup", "w_a", "w_b", "w_o", "w_up", "w_dn", "vecP", "vecB")
    xs = [np.ascontiguousarray(x[c // 4, (c % 4) * T:(c % 4 + 1) * T, :]) for c in range(8)]
    if FUSED:
        nc = _get_prog(L)
        in_maps = []
        for c in range(8):
            m = dict(shared)
            m.update(tabs[c % 4])
            m["x_in"] = xs[c]
            in_maps.append(m)
        res = run_bass_kernel_spmd(nc, in_maps, core_ids=list(range(8)))
        xs = [np.asarray(res.results[c]["out"]) for c in range(8)]
    else:
        nc = _get_prog(1)
        for l in range(L):
            in_maps = []
            for c in range(8):
                m = {k: (np.ascontiguousarray(v[l:l + 1]) if k in per_layer else v) for k, v in shared.items()}
                m.update(tabs[c % 4])
                m["x_in"] = xs[c]
                in_maps.append(m)
            res = run_bass_kernel_spmd(nc, in_maps, core_ids=list(range(8)))
            xs = [np.ascontiguousarray(np.asarray(res.results[c]["out"])) for c in range(8)]
    outp = np.empty_like(x)
    for c in range(8):
        outp[c // 4, (c % 4) * T:(c % 4 + 1) * T, :] = xs[c]
    return outp
```

```python
import numpy as np
import ml_dtypes
import concourse.bass as bass
import concourse.mybir as mybir
from concourse.bass_utils import run_bass_kernel_spmd
from contextlib import ExitStack

F32 = mybir.dt.float32
BF16 = mybir.dt.bfloat16
ALU = mybir.AluOpType
AF = mybir.ActivationFunctionType

D = 1024
SEQ = 8192
T = 2048
NB = 4
EPS = 1e-6
NCOLA = 2112
KVROWS = 544
FUSED = True


class Op:
    __slots__ = ("eng", "name", "kw", "deps", "signal", "sigval", "grp", "kind", "ep")

    def __init__(self, eng, name, kw, kind="c"):
        self.ep = 0
        self.eng = eng
        self.name = name
        self.kw = kw
        self.deps = []
        self.signal = False
        self.sigval = 0
        self.grp = None
        self.kind = kind


class Grp:
    __slots__ = ("sem", "final")

    def __init__(self, sem):
        self.sem = sem
        self.final = 0


class Prog:
    ENGS = ("pe", "act", "dve", "pool", "sp")

    def __init__(self, nc):
        self.nc = nc
        self.streams = {e: [] for e in self.ENGS}
        self.lastw = {}
        self.readers = {}
        self.bar = []
        self.dcount = {}
        self.dlast = {}
        self.cccount = 0
        self.epoch = 0

    def _deps(self, o, reads, writes):
        deps = []
        for k in reads:
            w = self.lastw.get(k)
            if w is not None:
                deps.append(w)
        for k in writes:
            w = self.lastw.get(k)
            if w is not None:
                deps.append(w)
            deps.extend(self.readers.get(k, ()))
        deps.extend(self.bar)
        o.deps = deps
        for k in reads:
            self.readers.setdefault(k, []).append(o)
        for k in writes:
            self.lastw[k] = o
            self.readers[k] = []

    def op(self, eng, name, reads=(), writes=(), **kw):
        o = Op(eng, name, kw)
        o.ep = self.epoch
        self._deps(o, reads, writes)
        self.streams[eng].append(o)
        return o

    def group(self, sem):
        g = Grp(sem)
        return g

    def dma(self, queue, g, out, in_, reads=(), writes=()):
        o = Op(queue, "dma_start", dict(out=out, in_=in_), kind="d")
        o.grp = g
        self._deps(o, reads, writes)
        prev = self.dlast.get(g.sem)
        if prev is not None and prev.grp is not g:
            o.deps.append(prev)
        self.dcount[g.sem] = self.dcount.get(g.sem, 0) + 16
        g.final = self.dcount[g.sem]
        self.dlast[g.sem] = o
        self.streams[queue].append(o)
        return o

    def collective(self, args, reads=(), writes=(), **kw):
        o = Op("pool", "collective_compute", kw, kind="cc")
        o.grp = args
        self._deps(o, reads, writes)
        self.cccount += 1
        o.sigval = self.cccount
        self.streams["pool"].append(o)
        return o

    def barrier(self):
        b = []
        for e in self.ENGS:
            for o in reversed(self.streams[e]):
                if o.kind == "c" or o.kind == "cc":
                    b.append(o)
                    break
        b.extend(self.dlast.values())
        self.bar = b

    def lower(self, stack):
        nc = self.nc
        for e in self.ENGS:
            for o in self.streams[e]:
                for d in o.deps:
                    if d.kind == "c" and not (d.eng == "pe" and o.eng == "pe"):
                        d.signal = True
        sems = {}
        for e in ("pe", "act", "dve", "pool"):
            cnts = {}
            for o in self.streams[e]:
                if o.kind == "c" and o.signal:
                    cnts[o.ep] = cnts.get(o.ep, 0) + 1
                    o.sigval = cnts[o.ep]
                    if (e, o.ep) not in sems:
                        sems[(e, o.ep)] = stack.enter_context(nc.semaphore("s_%s%d" % (e, o.ep)))
        sems["cc"] = stack.enter_context(nc.semaphore("s_cc"))
        dsems = {}
        for name in self.dcount:
            dsems[name] = stack.enter_context(nc.semaphore("d_" + name))
        block = stack.enter_context(nc.Block())

        def run(ename, eng):
            waited = {}
            for o in self.streams[ename]:
                need = {}
                for d in o.deps:
                    if d.kind == "d":
                        if o.kind == "d" and d.grp is o.grp:
                            continue
                        key, val = ("d", d.grp.sem), d.grp.final
                    elif d.kind == "cc":
                        key, val = ("cc",), d.sigval
                    else:
                        if d.eng == "pe" and ename == "pe":
                            continue
                        key, val = ("c", d.eng, d.ep), d.sigval
                    if need.get(key, 0) < val:
                        need[key] = val
                for key, val in need.items():
                    if waited.get(key, 0) >= val:
                        continue
                    waited[key] = val
                    if key[0] == "d":
                        eng.wait_ge(dsems[key[1]], val)
                    elif key[0] == "cc":
                        eng.wait_ge(sems["cc"], val)
                    else:
                        eng.wait_ge(sems[(key[1], key[2])], val)
                if o.name is None:
                    continue
                if o.kind == "cc":
                    ins = eng.collective_compute(*o.grp, **o.kw)
                else:
                    ins = getattr(eng, o.name)(**o.kw)
                if o.kind == "d":
                    ins.then_inc(dsems[o.grp.sem], 16)
                elif o.kind == "cc":
                    ins.then_inc(sems["cc"], 1)
                elif o.signal:
                    ins.then_inc(sems[(o.eng, o.ep)], 1)

        @block.tensor
        def _(eng):
            run("pe", eng)

        @block.scalar
        def _(eng):
            run("act", eng)

        @block.vector
        def _(eng):
            run("dve", eng)

        @block.gpsimd
        def _(eng):
            run("pool", eng)

        @block.sync
        def _(eng):
            run("sp", eng)


def build_program(depth, stop=None):
    nc = bass.Bass("TRN2", target_bir_lowering=False)
    P = Prog(nc)

    def din(name, shape, dt=F32):
        return nc.dram_tensor(name, shape, dt, kind="ExternalInput")

    x_in = din("x_in", [T, D])
    w_inA = din("w_inA", [depth, D, NCOLA])
    w_g = din("w_g", [depth, D, 2048])
    w_qup = din("w_qup", [depth, 384, 1536])
    w_kvup = din("w_kvup", [depth, 256, 1024])
    w_a = din("w_a", [depth, 512, 1024])
    w_b = din("w_b", [depth, 512, 1024])
    w_o = din("w_o", [depth, D, D])
    w_up = din("w_up", [depth, D, 4096])
    w_dn = din("w_dn", [depth, 4096, D])
    vecP = din("vecP", [depth, 128, 48])
    vecB = din("vecB", [depth, 2, D])
    tabA = din("tabA", [2, 128, T])
    tabB = din("tabB", [2, 32, T])
    cst = din("cst", [3, 128, 128], BF16)
    identd = din("identd", [128, 128], BF16)
    out = nc.dram_tensor("out", [T, D], F32, kind="ExternalOutput")

    x_cur = nc.dram_tensor("x_cur", [T, D], F32)
    uT_scr = nc.dram_tensor("uT_scr", [D, T], BF16)
    KVR = (256, 160, 128)
    kvf_own = [[nc.dram_tensor("kvf_own%d_%d" % (i, q), [KVR[q], T], BF16) for q in range(3)] for i in range(2)]
    kvf_all = [[nc.dram_tensor("kvf_all%d_%d" % (i, q), [4 * KVR[q], T], BF16) for q in range(3)] for i in range(2)]

    stack = ExitStack()
    NBF = 105472
    SB = stack.enter_context(nc.sbuf_tensor("sb", [128, NBF], BF16))
    PS = [stack.enter_context(nc.psum_tensor("ps%d" % i, [128, 1024], F32)) for i in range(4)]

    def v16(off, n, p0=0, p1=128):
        assert off % 4 == 0 and off + 2 * n <= NBF * 2, (off, n)
        return SB[p0:p1, off // 2: off // 2 + n]

    def v32(off, n, p0=0, p1=128):
        assert off % 4 == 0 and off + 4 * n <= NBF * 2, (off, n)
        return SB[p0:p1, off // 2: off // 2 + 2 * n].bitcast(F32)

    def bank(b, p0=0, p1=128):
        return PS[b // 2][p0:p1, (b % 2) * 512:(b % 2 + 1) * 512]

    K = 1024
    C_ONES = 0
    C_BLK = 512
    C_ID = 1024
    C_EXP = 1280
    C_VEC = 3328
    C_BH = 3520
    C_GB = 3584
    C_END = 12 * K
    ones_f = v16(C_ONES, 128)
    blk_f = v16(C_BLK, 128)
    ident = v16(C_ID, 128)
    expt = v32(C_EXP, 512)
    vec = v32(C_VEC, 48)
    bhalf = v32(C_BH, 16)
    gpost = [v32(C_GB + i * 4096, 1024) for i in range(2)]

    g0 = P.group("const")
    P.dma("sp", g0, ones_f, cst[0, :, :], writes=["ones"])
    P.dma("sp", g0, blk_f, cst[1, :, :], writes=["blk"])
    P.dma("sp", g0, ident, identd[:, :], writes=["ident"])
    P.op("dve", "memset", writes=["expt"], ap=expt, constant=-0.5)
    epsb = v32(C_GB + 8192, 16)
    P.op("dve", "memset", writes=["epsb"], ap=epsb, constant=EPS)

    O_Y = 12 * K
    O_QB = 44 * K
    O_QA = 76 * K

    def YA(p0, p1, pc, c0, c1):
        return SB[p0:p1, (O_Y // 2 + pc * T + c0):(O_Y // 2 + pc * T + c1)]

    def YB(p0, p1, pc, c0, c1):
        return SB[p0:p1, (O_Y // 2 + 4 * T + pc * T + c0):(O_Y // 2 + 4 * T + pc * T + c1)]

    def QB(p0, p1, h, c0, c1):
        return SB[p0:p1, (O_QB // 2 + h * T + c0):(O_QB // 2 + h * T + c1)]

    def QA(p0, p1, pc, c0, c1):
        return SB[p0:p1, (O_QA // 2 + pc * T + c0):(O_QA // 2 + pc * T + c1)]

    psrr = [0]

    def nbank():
        b = psrr[0] % 8
        psrr[0] += 1
        return b

    pprr = [0]

    def npair():
        p = pprr[0] % 4
        pprr[0] += 1
        return p

    def mm(outap, lhsT, rhs, start, stop, reads, writes):
        return P.op("pe", "matmul", reads=reads, writes=writes, out=outap, lhsT=lhsT, rhs=rhs, start=start, stop=stop)

    def rms_tokens(xb_f, xkey, nt, ss, mse, rstd, junk, eps, key):
        for j in range(nt):
            P.op("act", "activation", reads=[xkey], writes=[(key, "ss", j), (key, "junk")], out=junk, in_=xb_f[:, j, :], func=AF.Square,
                                                    accum_out=ss[:, j:j + 1])
        P.op("dve", "tensor_scalar", reads=[(key, "ss", j) for j in range(nt)], writes=[(key, "mse")], out=mse[:, 0:nt], in0=ss[:, 0:nt], scalar1=1.0 / D, scalar2=eps,
                                              op0=ALU.mult, op1=ALU.add)
        P.op("pool", "tensor_tensor", reads=[(key, "mse"), "expt"], writes=[(key, "rstd")], out=rstd[:, 0:nt], in0=mse[:, 0:nt], in1=expt[:, 0:nt], op=ALU.pow)

    for l in range(depth):
        P.epoch = l
        par = l % 2
        x_src = x_in if l == 0 else x_cur
        x_dst_final = out if (l == depth - 1) else x_cur
        P.barrier()
        A_T = 12 * K
        A_WA = 92 * K
        A_WQ = 126 * K
        A_XB = 136 * K
        A_UT = 168 * K
        A_TB = 184 * K
        A_UR = 192 * K
        A_ST = 196 * K
        Wa = v16(A_WA, 8 * NCOLA).rearrange("p (k n) -> p k n", k=8)
        Wq = v16(A_WQ, 3 * 1536).rearrange("p (k n) -> p k n", k=3)
        for q, (ca, cb) in enumerate(((1024, NCOLA), (0, 1024))):
            P.dma("pool", P.group("w%d" % q), Wa[:, :, ca:cb], w_inA[l].rearrange("(k p) n -> p k n", p=128)[:, :, ca:cb],
                  writes=[("Wa", q)])
        P.dma("pool", P.group("w2"), Wq, w_qup[l].rearrange("(k p) n -> p k n", p=128), writes=["Wq"])
        gv = P.group("vec")
        P.dma("sp", gv, vec, vecP[l, :, :], writes=["vec"])
        for i in range(2):
            P.dma("sp", gv, gpost[i], bass.AP(tensor=vecB, offset=(l * 2 + i) * D, ap=[[0, 128], [1, D]]),
                  writes=[("gpost", i)])
        P.op("dve", "tensor_scalar", reads=["vec"], writes=["bhalf"], out=bhalf, in0=vec[:, 25:41], scalar1=0.5, scalar2=None, op0=ALU.mult)
        t_ss = v32(A_T, 8)
        t_mse = v32(A_T + 64, 8)
        t_rstd = v32(A_T + 128, 8)
        t_junk = v16(A_T + 256, 1024)
        t_sq = [v16(A_T + 4 * K + i * 2 * K, 512) for i in range(3)]
        t_ms = [v32(A_T + 10 * K + i * 2 * K, 512) for i in range(2)]
        t_rs = [v32(A_T + 14 * K + i * 2 * K, 512) for i in range(2)]
        t_1 = [v32(A_T + 18 * K + i * 2 * K, 512) for i in range(2)]
        t_2 = [v32(A_T + 22 * K + i * 2 * K, 512) for i in range(2)]
        t_cqn = v16(A_T + 26 * K, 3 * 512).rearrange("p (c t) -> p c t", c=3)
        cnt = {"sq": 0, "ms": 0, "t": 0}
        cosA = v32(A_TB, 512)
        sinA = v32(A_TB + 2 * K, 512)
        cosB = lambda p0, p1: v32(A_TB + 4 * K, 512, p0, p1)
        sinB = lambda p0, p1: v32(A_TB + 6 * K, 512, p0, p1)
        st_k = v16(A_ST, 512)
        st_ckv = v16(A_ST + 1 * K, 2 * 512).rearrange("p (c t) -> p c t", c=2)
        st_kr = v16(A_ST + 3 * K, 512, 0, 32)
        st_v = v16(A_ST + 4 * K, 4 * 128).rearrange("p (j c) -> p j c", j=4)
        kvo = kvf_own[par]
        kva = kvf_all[par]

        def a_bufs(tb):
            xb = v32(A_XB + (tb % 2) * 16 * K, 4 * 1024).rearrange("p (j d) -> p j d", j=4)
            uT = v16(A_UT + (tb % 2) * 8 * K, 8 * 512).rearrange("p (k t) -> p k t", k=8)
            uk = ("uT", tb % 2)
            ukeys = [(uk, j, c) for j in range(4) for c in range(8)]
            return xb, ("xb", tb % 2), uT, uk, ukeys

        def a_load(tb):
            xb, xk, uT, uk, ukeys = a_bufs(tb)
            P.dma("sp", P.group("xb%d" % (tb % 2)), xb, x_src[tb * 512:(tb + 1) * 512, :].rearrange("(j p) d -> p j d", p=128),
                  reads=[("xcur", tb)], writes=[xk])

        def a_tables(tb):
            c0, c1 = tb * 512, (tb + 1) * 512
            gt = P.group("tab")
            P.dma("sp", gt, cosA, tabA[0, :, c0:c1], writes=["tab"])
            P.dma("sp", gt, sinA, tabA[1, :, c0:c1], writes=["tab"])
            for (p0, p1) in ((0, 32), (64, 96)):
                P.dma("sp", gt, cosB(p0, p1), tabB[0, :, c0:c1], writes=["tab"])
                P.dma("sp", gt, sinB(p0, p1), tabB[1, :, c0:c1], writes=["tab"])

        def a_norm(tb):
            c0, c1 = tb * 512, (tb + 1) * 512
            xb, xk, uT, uk, ukeys = a_bufs(tb)
            rms_tokens(xb, xk, 4, t_ss, t_mse, t_rstd, t_junk, EPS, "nA")
            for j in range(4):
                ur = v16(A_UR + (j % 2) * 2 * K, 1024)
                urk = ("ur", j % 2)
                P.op("dve", "tensor_scalar", reads=[xk, ("nA", "rstd")], writes=[urk], out=ur, in0=xb[:, j, :],
                     scalar1=t_rstd[:, j:j + 1], scalar2=None, op0=ALU.mult)
                b = nbank()
                pT = bank(b).bitcast(BF16).rearrange("p (c t) -> p c t", t=128)
                for c in range(8):
                    P.op("pe", "transpose", reads=[urk, "ident"], writes=[("ps", b)], out=pT[:, c, :],
                         in_=ur[:, c * 128:(c + 1) * 128], identity=ident)
                for c in range(8):
                    P.op("dve", "tensor_scalar", reads=[("ps", b), "vec"], writes=[(uk, j, c)],
                         out=uT[:, c, j * 128:(j + 1) * 128], in0=pT[:, c, :], scalar1=vec[:, c:c + 1], scalar2=None,
                         op0=ALU.mult)
            P.dma("sp", P.group("ust%d" % (tb % 2)), uT_scr.ap().rearrange("(k p) t -> p k t", p=128)[:, :, c0:c1], uT,
                  reads=ukeys, writes=[("uscr", tb)])

        def a_helpers(tb):
            xb, xk, uT, uk, ukeys = a_bufs(tb)
            def proj(col0, M, b, p0=0):
                for k in range(8):
                    wk = ("Wa", 0 if col0 >= 1024 else 1)
                    mm(bank(b, p0, p0 + M), Wa[:, k, col0:col0 + M], uT[:, k, :], k == 0, k == 7,
                       reads=(ukeys if k in (0, 7) else []) + [wk], writes=[("ps", b)])

            def stat(sqs, blkones, scale_eps, nparts=128):
                b = nbank()
                n = len(sqs)
                for i, (sq, sk) in enumerate(sqs):
                    mm(bank(b), blkones, sq, i == 0, i == n - 1, reads=[sk, "ones", "blk"], writes=[("ps", b)])
                mi = cnt["ms"] % 2
                cnt["ms"] += 1
                ms, rs = t_ms[mi], t_rs[mi]
                P.op("act", "activation", reads=[("ps", b)], writes=[("ms", mi)], out=ms, in_=bank(b), func=AF.Sqrt,
                     scale=scale_eps[0], bias=epsb[:, 0:1])
                P.op("dve", "reciprocal", reads=[("ms", mi)], writes=[("rs", mi)], out=rs, in_=ms)
                return rs, ("rs", mi)

            def square_of(b):
                si = cnt["sq"] % 3
                cnt["sq"] += 1
                sq = t_sq[si]
                P.op("act", "activation", reads=[("ps", b)], writes=[("sq", si)], out=sq, in_=bank(b), func=AF.Square)
                return sq, ("sq", si)

            def rope_combine(b1, b2, gcol, gpcol, cs, sn, outap, outkey, rs=None, rsk=None, p0=0, p1=128):
                ti = cnt["t"] % 2
                cnt["t"] += 1
                a1 = v32(A_T + 18 * K + ti * 2 * K, 512, p0, p1)
                a2 = v32(A_T + 22 * K + ti * 2 * K, 512, p0, p1)
                if gcol is None:
                    P.op("dve", "tensor_tensor", reads=[("ps", b1), "tab"], writes=[("t1", ti)], out=a1, in0=bank(b1, p0, p1), in1=cs, op=ALU.mult)
                    P.op("dve", "tensor_tensor", reads=[("ps", b2), "tab"], writes=[("t2", ti)], out=a2, in0=bank(b2, p0, p1), in1=sn, op=ALU.mult)
                else:
                    P.op("dve", "scalar_tensor_tensor", reads=[("ps", b1), "tab", "vec"], writes=[("t1", ti)], out=a1, in0=bank(b1, p0, p1), scalar=vec[p0:p1, gcol:gcol + 1],
                                                                 in1=cs, op0=ALU.mult, op1=ALU.mult)
                    P.op("dve", "scalar_tensor_tensor", reads=[("ps", b2), "tab", "vec"], writes=[("t2", ti)], out=a2, in0=bank(b2, p0, p1), scalar=vec[p0:p1, gpcol:gpcol + 1],
                                                                 in1=sn, op0=ALU.mult, op1=ALU.mult)
                if rs is None:
                    P.op("pool", "tensor_tensor", reads=[("t1", ti), ("t2", ti)], writes=[outkey], out=outap, in0=a1, in1=a2, op=ALU.add)
                else:
                    P.op("pool", "tensor_tensor", reads=[("t1", ti), ("t2", ti)], writes=[("t1", ti)], out=a1, in0=a1, in1=a2, op=ALU.add)
                    P.op("pool", "tensor_tensor", reads=[("t1", ti), rsk], writes=[outkey], out=outap, in0=a1, in1=rs, op=ALU.mult)

            return proj, stat, square_of, rope_combine

        def a_kv(tb):
            c0, c1 = tb * 512, (tb + 1) * 512
            xb, xk, uT, uk, ukeys = a_bufs(tb)
            proj, stat, square_of, rope_combine = a_helpers(tb)
            gk = P.group("kvst")
            b1, b2 = nbank(), nbank()
            proj(1024, 128, b1)
            proj(1152, 128, b2)
            sq, sk = square_of(b1)
            rs, rsk = stat([(sq, sk)], blk_f, (1.0 / 64, EPS))
            rope_combine(b1, b2, 23, 24, cosA, sinA, st_k, ("st", "k"), rs, rsk)
            P.dma("sp", gk, kvo[1][0:128, c0:c1], st_k, reads=[("st", "k")], writes=[("kvown", par)])
            bs = [nbank() for _ in range(2)]
            sqs = []
            for c in range(2):
                proj(1664 + c * 128, 128, bs[c])
                sqs.append(square_of(bs[c]))
            rs, rsk = stat(sqs, ones_f, (1.0 / 256, EPS))
            for c in range(2):
                P.op("dve", "scalar_tensor_tensor", reads=[("ps", bs[c]), rsk, "vec"], writes=[("st", "ckv", c)], out=st_ckv[:, c, :], in0=bank(bs[c]),
                                                                       scalar=vec[:, 19 + c:20 + c], in1=rs,
                                                                       op0=ALU.mult, op1=ALU.mult)
            P.dma("sp", gk, kvo[0].ap()[0:256, c0:c1].rearrange("(c p) t -> p c t", p=128), st_ckv,
                  reads=[("st", "ckv", 0), ("st", "ckv", 1)], writes=[("kvown", par)])
            b1, b2 = nbank(), nbank()
            proj(1920, 32, b1)
            proj(1952, 32, b2)
            rope_combine(b1, b2, None, None, cosB(0, 32), sinB(0, 32), st_kr, ("st", "kr"), p0=0, p1=32)
            P.dma("sp", gk, kvo[1][128:160, c0:c1], st_kr, reads=[("st", "kr")], writes=[("kvown", par)])
            b = nbank()
            for j in range(4):
                for k in range(8):
                    mm(bank(b)[:, j * 128:(j + 1) * 128], uT[:, k, j * 128:(j + 1) * 128], Wa[:, k, 1984:2112],
                       k == 0, k == 7, reads=(ukeys if k in (0, 7) else []) + [("Wa", 0)], writes=[("ps", b)])
            P.op("act", "copy", reads=[("ps", b)], writes=[("st", "v")], out=st_v, in_=bank(b).rearrange("p (j c) -> p j c", j=4))
            vdst = bass.AP(tensor=kvo[2], offset=c0 * 128, ap=[[128, 128], [128 * 128, 4], [1, 128]])
            P.dma("sp", gk, vdst, st_v, reads=[("st", "v")], writes=[("kvown", par)])


        def a_q(tb):
            c0, c1 = tb * 512, (tb + 1) * 512
            xb, xk, uT, uk, ukeys = a_bufs(tb)
            proj, stat, square_of, rope_combine = a_helpers(tb)
            for pc in range(4):
                b1, b2 = nbank(), nbank()
                proj(pc * 128, 128, b1)
                proj(512 + pc * 128, 128, b2)
                sq, sk = square_of(b1)
                rs, rsk = stat([(sq, sk)], blk_f, (1.0 / 64, EPS))
                rope_combine(b1, b2, 21, 22, cosA, sinA, QA(0, 128, pc, c0, c1), ("QA", pc, tb), rs, rsk)
            bs = [nbank() for _ in range(3)]
            sqs = []
            for c in range(3):
                proj(1280 + c * 128, 128, bs[c])
                sqs.append(square_of(bs[c]))
            rs, rsk = stat(sqs, ones_f, (1.0 / 384, EPS))
            for c in range(3):
                P.op("dve", "scalar_tensor_tensor", reads=[("ps", bs[c]), rsk, "vec"], writes=[("cqn", c)], out=t_cqn[:, c, :], in0=bank(bs[c]),
                                                                       scalar=vec[:, 16 + c:17 + c], in1=rs,
                                                                       op0=ALU.mult, op1=ALU.mult)
            for h in range(8):
                b1, b2 = nbank(), nbank()
                for c in range(3):
                    mm(bank(b1, 0, 96), Wq[:, c, h * 192:h * 192 + 96], t_cqn[:, c, :], c == 0, c == 2,
                       reads=[("cqn", c), "Wq"], writes=[("ps", b1)])
                for c in range(3):
                    mm(bank(b2, 0, 96), Wq[:, c, h * 192 + 96:h * 192 + 192], t_cqn[:, c, :], c == 0, c == 2,
                       reads=[("cqn", c), "Wq"], writes=[("ps", b2)])
                P.op("act", "copy", reads=[("ps", b1)], writes=[("QB", h, tb, 0)], out=QB(0, 64, h, c0, c1), in_=bank(b1, 0, 64))
                rope_combine(b1, b2, None, None, cosB(64, 96), sinB(64, 96), QB(64, 96, h, c0, c1), ("QB", h, tb, 1),
                             p0=64, p1=96)

        def a_reload(tb):
            xb, xk, uT, uk, ukeys = a_bufs(tb)
            P.dma("sp", P.group("xb%d" % (tb % 2)), uT,
                  uT_scr.ap().rearrange("(k p) t -> p k t", p=128)[:, :, tb * 512:(tb + 1) * 512],
                  reads=[("uscr", tb)], writes=ukeys)

        if stop != "A0":
            a_load(0)
            a_norm(0)
            for tb in range(NB):
                if tb + 1 < NB:
                    a_load(tb + 1)
                a_tables(tb)
                a_kv(tb)
                if tb + 1 < NB:
                    a_norm(tb + 1)
            if stop not in ("A", "A1"):
                for q in range(3):
                    P.collective(("AllGather", ALU.bypass), reads=[("kvown", par)], writes=[("kvall", par, q)],
                                 replica_groups=[[0, 1, 2, 3], [4, 5, 6, 7]],
                                 ins=[kvf_own[par][q].ap().opt()], outs=[kva[q].ap().opt()])
            a_reload(0)
            for tb in range(NB):
                if tb + 1 < NB:
                    a_reload(tb + 1)
                a_tables(tb)
                a_q(tb)

        if stop in ("A", "A0", "A1"):
            break
        if stop == "AG":
            break
        P.barrier()
        T_KB = [76 * K, 92 * K]
        T_VB = [108 * K, 108 * K + 8320]
        T_KA = 92 * K
        T_VA = 108 * K
        T_CK = 126 * K
        T_P = 158 * K
        T_YA = 164 * K
        T_RC = 172 * K
        T_WKV = 180 * K
        KA = v16(T_KA, SEQ)
        VA = v16(T_VA, 64 * 130).rearrange("p (t g d) -> p t g d", t=64, g=2)
        CK = v16(T_CK, 2 * SEQ).rearrange("p (c t) -> p c t", c=2)
        Wkv = v16(T_WKV, 2 * 1024).rearrange("p (c n) -> p c n", c=2)
        ga = P.group("attA")
        for r in range(4):
            P.dma("sp", ga, KA[:, r * T:(r + 1) * T], kva[1][r * 160:r * 160 + 128, :],
                  reads=[("kvall", par, 1)], writes=["KA"])
            for g in range(2):
                vsrc = bass.AP(tensor=kva[2], offset=r * 128 * T + g * 64, ap=[[128, 128], [128 * 128, 16], [1, 64]])
                P.dma("sp", ga, VA[:, r * 16:(r + 1) * 16, g, 0:64], vsrc, reads=[("kvall", par, 2)], writes=["VA"])
        P.op("dve", "memset", writes=["VA1"], ap=VA[:, :, :, 64:65], constant=1.0)
        gc = P.group("attC")
        for r in range(4):
            P.dma("sp", gc, CK[:, :, r * T:(r + 1) * T],
                  kva[0].ap()[r * 256:(r + 1) * 256, :].rearrange("(c p) t -> p c t", p=128),
                  reads=[("kvall", par, 0)], writes=["CK"])
        gwk = P.group("w7")
        P.dma("pool", gwk, Wkv, w_kvup[l].rearrange("(c p) n -> p c n", p=128), writes=["Wkv"])

        Sb = [PS[0], PS[1]]
        pend = []
        state = {"g": 0, "acc": 0}

        def attention(jobs):
            units = []
            for jb in jobs:
                for u in range(jb["n"]):
                    units.append((jb, u))
            n = len(units)
            jobset = {}
            for ji, jb in enumerate(jobs):
                jobset[id(jb)] = (state["acc"] + ji) % 2
            state["acc"] += len(jobs)

            def accbank(jb, i):
                return 4 + 2 * jobset[id(jb)] + i

            def qk(ui):
                jb, u = units[ui]
                s_ = state["g"] + ui
                for t, (lh, rh) in enumerate(jb["qk"](u)):
                    mm(Sb[s_ % 2][:, t * 512:(t + 1) * 512], lh, rh, True, True,
                       reads=jb["rk"], writes=[("S", s_ % 2)])

            def ex(ui):
                jb, u = units[ui]
                s_ = state["g"] + ui
                pt = v16(T_P + (s_ % 3) * 2 * K, 1024)
                P.op("act", "activation", reads=[("S", s_ % 2)], writes=[("P", s_ % 3)], out=pt, in_=Sb[s_ % 2][:, :],
                     func=AF.Exp, scale=jb["scale"])

            def pv(ui):
                jb, u = units[ui]
                s_ = state["g"] + ui
                pt = v16(T_P + (s_ % 3) * 2 * K, 1024)
                for t, (ai, lh, st, sp) in enumerate(jb["pv"](u)):
                    bk = accbank(jb, ai)
                    mm(bank(bk, 0, 65), lh, pt[:, t * 512:(t + 1) * 512], st, sp,
                       reads=[("P", s_ % 3)] + jb["vk"], writes=[("ps", bk)])
                if u == jb["n"] - 1:
                    for ai in range(jb["nacc"]):
                        bk = accbank(jb, ai)
                        slot = bk - 4
                        ya_off = T_YA + slot * 2 * K
                        rc_off = T_RC + (slot % 2) * 4 * K
                        ya = v32(ya_off, 512, 0, 65)
                        rc = v32(rc_off, 512, 64, 65)
                        rch = v16(rc_off + 2 * K, 512, 64, 65)
                        rcl = v16(rc_off + 3 * K, 512, 64, 65)
                        P.op("dve", "tensor_copy", reads=[("ps", bk)], writes=[("ya", slot)], out=ya, in_=bank(bk, 0, 65))
                        P.op("dve", "reciprocal", reads=[("ya", slot)], writes=[("rc", slot % 2)], out=rc,
                             in_=v32(ya_off, 512, 64, 65))
                        P.op("dve", "tensor_copy", reads=[("rc", slot % 2)], writes=[("rch", slot % 2)], out=rch, in_=rc)
                        P.op("dve", "tensor_tensor", reads=[("rc", slot % 2), ("rch", slot % 2)], writes=[("rcl", slot % 2)],
                             out=rcl, in0=rc, in1=rch, op=ALU.subtract)
                        dst = jb["dst"][ai]
                        yk = jb["ykeys"][ai]

                        def stage2(bk=bk, slot=slot, ya_off=ya_off, rch=rch, rcl=rcl, dst=dst, yk=yk):
                            mm(bank(bk, 0, 64), ones_f[64:65, 0:64], rch, True, False, reads=[("rch", slot % 2), "ones"],
                               writes=[("ps", bk)])
                            mm(bank(bk, 0, 64), ones_f[64:65, 0:64], rcl, False, True, reads=[("rcl", slot % 2), "ones"],
                               writes=[("ps", bk)])
                            P.op("dve", "tensor_tensor", reads=[("ya", slot), ("ps", bk)], writes=[yk], out=dst,
                                 in0=v32(ya_off, 512, 0, 64), in1=bank(bk, 0, 64), op=ALU.mult)
                        pend.append([8 + 6 * ai, stage2])

            for ui in range(n + 2):
                if ui < n:
                    qk(ui)
                    ex(ui)
                if ui >= 2:
                    pv(ui - 2)
                for pp in list(pend):
                    pp[0] -= 1
                    if pp[0] <= 0:
                        pp[1]()
                        pend.remove(pp)
                if ui < n:
                    jb, u = units[ui]
                    exl = jb.get("extra", [])
                    if exl and u >= 12 and (u - 12) % 3 == 0 and (u - 12) // 3 < len(exl):
                        exl[(u - 12) // 3]()
            for pp in list(pend):
                pp[1]()
                pend.remove(pp)
            state["g"] += n

        jobs = []
        for pc in range(4):
            for qc in range(4):
                jobs.append(dict(
                    n=64, nacc=2, scale=1.0 / 8.0,
                    qk=lambda u, pc=pc, qc=qc: [(KA[g * 64:(g + 1) * 64, u * 128:(u + 1) * 128],
                                                 QA(g * 64, (g + 1) * 64, pc, qc * 512, (qc + 1) * 512)) for g in range(2)],
                    pv=lambda u: [(g, VA[:, u, g, :], u == 0, u == 63) for g in range(2)],
                    dst=[YA(g * 64, (g + 1) * 64, pc, qc * 512, (qc + 1) * 512) for g in range(2)],
                    ykeys=[("YA", g, pc, qc) for g in range(2)],
                    rk=["KA"] + [("QA", pc, tb) for tb in range(4)], vk=["VA", "VA1"]))
        attention(jobs)

        if stop == "GQA":
            break
        P.barrier()
        KBt = [v16(T_KB[i], SEQ, 0, 96) for i in range(2)]
        VBt = [v16(T_VB[i], 64 * 65).rearrange("p (t d) -> p t d", t=64) for i in range(2)]
        gkr = P.group("attK")
        for i in range(2):
            for r in range(4):
                P.dma("sp", gkr, v16(T_KB[i], SEQ, 64, 96)[:, r * T:(r + 1) * T], kva[1][r * 160 + 128:r * 160 + 160, :],
                      reads=[("kvall", par, 1)], writes=[("KBr", i)])
            P.op("dve", "memset", writes=[("VB1", i)], ap=VBt[i][:, :, 64:65], constant=1.0)

        def expansion_units(h):
            i = h % 2
            us = []
            for ch in range(16):
                def f(ch=ch):
                    b = 7 if ch % 2 == 0 else 5
                    for c in range(2):
                        mm(bank(b, 0, 64), Wkv[:, c, h * 128:h * 128 + 64], CK[:, c, ch * 512:(ch + 1) * 512], c == 0, c == 1,
                           reads=["Wkv", "CK"], writes=[("ps", b)])
                    P.op("dve", "tensor_copy", reads=[("ps", b)], writes=[("KB", i)], out=v16(T_KB[i], SEQ, 0, 64)[:, ch * 512:(ch + 1) * 512],
                                                        in_=bank(b, 0, 64))
                us.append(f)
            for t8 in range(8):
                def f(t8=t8):
                    b = 7 if t8 % 2 == 0 else 5
                    for tt in range(8):
                        kt = t8 * 8 + tt
                        for c in range(2):
                            mm(bank(b)[:, tt * 64:(tt + 1) * 64], CK[:, c, kt * 128:(kt + 1) * 128],
                               Wkv[:, c, h * 128 + 64:h * 128 + 128], c == 0, c == 1,
                               reads=["Wkv", "CK"], writes=[("ps", b)])
                    P.op("dve", "tensor_copy", reads=[("ps", b)], writes=[("VB", i)], out=VBt[i][:, t8 * 8:(t8 + 1) * 8, 0:64],
                                                        in_=bank(b).rearrange("p (t d) -> p t d", t=8))
                us.append(f)
            return us

        for f in expansion_units(0):
            f()
        jobs = []
        for h in range(8):
            i = h % 2
            for qc in range(4):
                jobs.append(dict(
                    n=32, nacc=1, scale=1.0 / float(np.sqrt(96.0)),
                    qk=lambda u, i=i, h=h, qc=qc: [(KBt[i][:, (2 * u + t) * 128:(2 * u + t + 1) * 128],
                                                   QB(0, 96, h, qc * 512, (qc + 1) * 512)) for t in range(2)],
                    pv=lambda u, i=i: [(0, VBt[i][:, 2 * u + t, :], 2 * u + t == 0, 2 * u + t == 63) for t in range(2)],
                    dst=[YB((h % 2) * 64, (h % 2) * 64 + 64, h // 2, qc * 512, (qc + 1) * 512)],
                    ykeys=[("YB", h, qc)],
                    rk=[("KB", i), ("KBr", i)] + [("QB", h, tb, s_) for tb in range(4) for s_ in range(2)],
                    vk=[("VB", i), ("VB1", i)],
                    extra=(expansion_units(h + 1)[qc * 6:(qc + 1) * 6] if h < 7 else [])))
        attention(jobs)

        if stop == "MLA":
            break
        P.barrier()
        M_WG = 44 * K
        M_WAB = 76 * K
        M_WO = 92 * K
        M_UT = 108 * K
        M_XB = 124 * K
        M_T = 156 * K
        Wg = v16(M_WG, 8 * 2048).rearrange("p (k n) -> p k n", k=8)
        Wab = [v16(M_WAB + i * 8 * K, 4 * 1024).rearrange("p (k n) -> p k n", k=4) for i in range(2)]
        Wo = v16(M_WO, 8 * 1024).rearrange("p (k n) -> p k n", k=8)
        P.dma("pool", P.group("w0"), Wg[:, :, 0:256], w_g[l].rearrange("(k p) n -> p k n", p=128)[:, :, 0:256], writes=[("Wg", 0)])
        P.dma("pool", P.group("w1"), Wg[:, :, 1024:1280], w_g[l].rearrange("(k p) n -> p k n", p=128)[:, :, 1024:1280], writes=[("Wg", 1)])
        gm = P.group("w2")
        P.dma("pool", gm, Wab[0], w_a[l].rearrange("(k p) n -> p k n", p=128), writes=["Wab"])
        P.dma("pool", gm, Wab[1], w_b[l].rearrange("(k p) n -> p k n", p=128), writes=["Wab"])
        P.dma("pool", P.group("w3"), Wg[:, :, 256:1024], w_g[l].rearrange("(k p) n -> p k n", p=128)[:, :, 256:1024], writes=[("Wg", 2)])
        P.dma("pool", P.group("w4"), Wg[:, :, 1280:2048], w_g[l].rearrange("(k p) n -> p k n", p=128)[:, :, 1280:2048], writes=[("Wg", 3)])
        P.dma("pool", P.group("w5"), Wo, w_o[l].rearrange("(k p) n -> p k n", p=128), writes=["Wo"])
        m_ta = [v32(M_T + i * 2 * K, 512) for i in range(2)]
        m_tb = [v32(M_T + 4 * K + i * 2 * K, 512) for i in range(2)]
        m_u1 = [v32(M_T + 8 * K + i * 2 * K, 512) for i in range(2)]
        m_u2 = [v32(M_T + 12 * K + i * 2 * K, 512) for i in range(2)]
        m_mTs = [v16(M_T + 16 * K, 8 * 512).rearrange("p (c t) -> p c t", c=8),
                 v16(M_T + 36 * K, 8 * 512).rearrange("p (c t) -> p c t", c=8)]
        m_tmp = [v32(M_T + 24 * K + i * 4 * K, 1024) for i in range(2)]
        m_ss = v32(M_T + 32 * K, 8)
        m_mse = v32(M_T + 32 * K + 64, 8)
        m_rstd = v32(M_T + 32 * K + 128, 8)
        m_junk = v16(M_T + 33 * K, 1024)
        def m_load(tb):
            c0, c1 = tb * 512, (tb + 1) * 512
            xb = v32(M_XB + (tb % 2) * 16 * K, 4 * 1024).rearrange("p (j d) -> p j d", j=4)
            uT = v16(M_UT + (tb % 2) * 8 * K, 8 * 512).rearrange("p (k t) -> p k t", k=8)
            uk = ("muT", tb % 2)
            m_mT = m_mTs[tb % 2]
            gx = P.group("mx%d" % (tb % 2))
            P.dma("sp", gx, xb, x_src[c0:c1, :].rearrange("(j p) d -> p j d", p=128),
                  reads=[("xcur", tb)], writes=[("mxb", tb % 2, j) for j in range(4)])
            P.dma("sp", gx, uT, uT_scr.ap().rearrange("(k p) t -> p k t", p=128)[:, :, c0:c1],
                  reads=[("uscr", tb)], writes=[uk])

        def m_cl(tb):
            c0, c1 = tb * 512, (tb + 1) * 512
            xb = v32(M_XB + (tb % 2) * 16 * K, 4 * 1024).rearrange("p (j d) -> p j d", j=4)
            uT = v16(M_UT + (tb % 2) * 8 * K, 8 * 512).rearrange("p (k t) -> p k t", k=8)
            uk = ("muT", tb % 2)
            m_mT = m_mTs[tb % 2]
            for c in range(8):
                i = c % 2
                bg1, bg2, ba, bb = nbank(), nbank(), nbank(), nbank()
                for k in range(8):
                    mm(bank(bg1), Wg[:, k, c * 128:(c + 1) * 128], uT[:, k, :], k == 0, k == 7,
                       reads=[("Wg", 0 if c < 2 else 2), uk], writes=[("ps", bg1)])
                for k in range(8):
                    mm(bank(bg2), Wg[:, k, 1024 + c * 128:1024 + (c + 1) * 128], uT[:, k, :], k == 0, k == 7,
                       reads=[("Wg", 1 if c < 2 else 3), uk], writes=[("ps", bg2)])
                for k in range(4):
                    mm(bank(ba), Wab[0][:, k, c * 128:(c + 1) * 128], YA(0, 128, k, c0, c1), k == 0, k == 3,
                       reads=["Wab"] + [("YA", g, k, tb) for g in range(2)], writes=[("ps", ba)])
                for k in range(4):
                    mm(bank(bb), Wab[1][:, k, c * 128:(c + 1) * 128], YB(0, 128, k, c0, c1), k == 0, k == 3,
                       reads=["Wab"] + [("YB", 2 * k + s, tb) for s in range(2)], writes=[("ps", bb)])
                P.op("act", "activation", reads=[("ps", bg1), "bhalf"], writes=[("mta", i)], out=m_ta[i], in_=bank(bg1), func=AF.Tanh,
                                                                     bias=bhalf[:, c:c + 1], scale=0.5)
                P.op("act", "activation", reads=[("ps", bg2), "bhalf"], writes=[("mtb", i)], out=m_tb[i], in_=bank(bg2), func=AF.Tanh,
                                                                     bias=bhalf[:, 8 + c:9 + c], scale=0.5)
                P.op("dve", "scalar_tensor_tensor", reads=[("mta", i), ("ps", ba)], writes=[("mu1", i)], out=m_u1[i], in0=m_ta[i], scalar=1.0, in1=bank(ba),
                                                                        op0=ALU.add, op1=ALU.mult)
                P.op("dve", "scalar_tensor_tensor", reads=[("mtb", i), ("ps", bb)], writes=[("mu2", i)], out=m_u2[i], in0=m_tb[i], scalar=1.0, in1=bank(bb),
                                                                        op0=ALU.add, op1=ALU.mult)
                P.op("dve", "tensor_tensor", reads=[("mu1", i), ("mu2", i)], writes=[("mT", tb % 2, c)], out=m_mT[:, c, :], in0=m_u1[i], in1=m_u2[i], op=ALU.add)

        def m_op(tb):
            c0, c1 = tb * 512, (tb + 1) * 512
            xb = v32(M_XB + (tb % 2) * 16 * K, 4 * 1024).rearrange("p (j d) -> p j d", j=4)
            uT = v16(M_UT + (tb % 2) * 8 * K, 8 * 512).rearrange("p (k t) -> p k t", k=8)
            uk = ("muT", tb % 2)
            m_mT = m_mTs[tb % 2]
            for j in range(4):
                pr = npair()
                for hf in range(2):
                    for c in range(8):
                        mm(PS[pr][:, hf * 512:(hf + 1) * 512], m_mT[:, c, j * 128:(j + 1) * 128],
                           Wo[:, c, hf * 512:(hf + 1) * 512], c == 0, c == 7,
                           reads=["Wo", ("mT", tb % 2, c)], writes=[("ps", 2 * pr + hf)])
                P.op("act", "activation", reads=[("ps", 2 * pr), ("ps", 2 * pr + 1)], writes=[("mss", j), "mjunk"], out=m_junk, in_=PS[pr][:, :], func=AF.Square,
                                                              accum_out=m_ss[:, j:j + 1])
                P.op("dve", "tensor_scalar", reads=[("mss", j)], writes=[("mmse", j)], out=m_mse[:, j:j + 1], in0=m_ss[:, j:j + 1], scalar1=1.0 / D,
                                                           scalar2=4.0 * EPS, op0=ALU.mult, op1=ALU.add)
                P.op("pool", "tensor_tensor", reads=[("mmse", j), "expt"], writes=[("mrstd", j)], out=m_rstd[:, j:j + 1], in0=m_mse[:, j:j + 1], in1=expt[:, 0:1],
                                                            op=ALU.pow)
                tmp = m_tmp[j % 2]
                P.op("dve", "tensor_tensor", reads=[("ps", 2 * pr), ("ps", 2 * pr + 1), ("gpost", 0)], writes=[("mtmp", j % 2)], out=tmp, in0=PS[pr][:, :], in1=gpost[0], op=ALU.mult)
                P.op("dve", "scalar_tensor_tensor", reads=[("mtmp", j % 2), ("mrstd", j), ("mxb", tb % 2, j)], writes=[("mxb", tb % 2, j)], out=xb[:, j, :], in0=tmp,
                                                                                 scalar=m_rstd[:, j:j + 1], in1=xb[:, j, :],
                                                                                 op0=ALU.mult, op1=ALU.add)
            gs = P.group("mst%d" % (tb % 2))
            P.dma("sp", gs, x_cur[c0:c1, :].rearrange("(j p) d -> p j d", p=128), xb,
                  reads=[("mxb", tb % 2, j) for j in range(4)], writes=[("xcur", tb)])

        m_load(0)
        m_cl(0)
        for tb in range(NB):
            if tb + 1 < NB:
                m_load(tb + 1)
                m_cl(tb + 1)
            m_op(tb)

        if stop == "M":
            break
        P.barrier()
        F_WU = 12 * K
        F_WD = 76 * K
        F_AT = 140 * K
        F_XB = 172 * K
        F_UT = 188 * K
        F_T = 196 * K
        Wu = v16(F_WU, 8 * 4096).rearrange("p (k n) -> p k n", k=8)
        Wd = v16(F_WD, 32 * 1024).rearrange("p (k n) -> p k n", k=32)
        for q in range(4):
            P.dma("pool", P.group("w%d" % q), Wu[:, :, q * 1024:(q + 1) * 1024],
                  w_up[l].rearrange("(k p) n -> p k n", p=128)[:, :, q * 1024:(q + 1) * 1024], writes=[("Wu", q)])
        for q in range(4):
            P.dma("pool", P.group("w%d" % (4 + q)), Wd[:, q * 8:(q + 1) * 8, :],
                  w_dn[l].rearrange("(k p) n -> p k n", p=128)[:, q * 8:(q + 1) * 8, :], writes=[("Wd", q)])
        FB = 8
        aTs = [v16(F_AT + i * 16 * K, 32 * 256).rearrange("p (k t) -> p k t", k=32) for i in range(2)]
        f_ur = [v16(F_T + i * 2 * K, 1024) for i in range(2)]
        f_r = [v32(F_T + 4 * K + i * 1 * K, 256) for i in range(4)]
        f_ss = v32(F_T + 8 * K, 8)
        f_mse = v32(F_T + 8 * K + 64, 8)
        f_rstd = v32(F_T + 8 * K + 128, 8)
        f_ss2 = v32(F_T + 8 * K + 192, 8)
        f_mse2 = v32(F_T + 8 * K + 256, 8)
        f_rstd2 = v32(F_T + 8 * K + 320, 8)
        f_junk = v16(F_T + 9 * K, 1024 // 2)
        xbs = [v32(F_XB + i * 8 * K, 2 * 1024).rearrange("p (j d) -> p j d", j=2) for i in range(2)]
        u2s = [v16(F_UT + i * 4 * K, 8 * 256).rearrange("p (k t) -> p k t", k=8) for i in range(2)]

        def f_load(tb):
            i = tb % 2
            P.dma("sp", P.group("fx%d" % i), xbs[i], x_cur[tb * 256:(tb + 1) * 256, :].rearrange("(j p) d -> p j d", p=128),
                  reads=[("xcur", tb // 2)], writes=[("fxb", i, j) for j in range(2)])

        def f_norm(tb):
            i = tb % 2
            xb, u2 = xbs[i], u2s[i]
            for j in range(2):
                for hf in range(2):
                    P.op("act", "activation", reads=[("fxb", i, j)], writes=[("fss", i, j, hf), "fjunk"], out=f_junk,
                         in_=xb[:, j, hf * 512:(hf + 1) * 512], func=AF.Square,
                         accum_out=f_ss[:, hf * 4 + i * 2 + j:hf * 4 + i * 2 + j + 1])
            P.op("dve", "tensor_tensor", reads=[("fss", i, j, hf) for j in range(2) for hf in range(2)], writes=[("fmse", i)],
                 out=f_mse[:, i * 2:i * 2 + 2], in0=f_ss[:, i * 2:i * 2 + 2], in1=f_ss[:, 4 + i * 2:4 + i * 2 + 2], op=ALU.add)
            P.op("dve", "tensor_scalar", reads=[("fmse", i)], writes=[("fmse", i)], out=f_mse[:, i * 2:i * 2 + 2],
                 in0=f_mse[:, i * 2:i * 2 + 2], scalar1=1.0 / D, scalar2=EPS, op0=ALU.mult, op1=ALU.add)
            P.op("pool", "tensor_tensor", reads=[("fmse", i), "expt"], writes=[("frstd", i)], out=f_rstd[:, i * 2:i * 2 + 2],
                 in0=f_mse[:, i * 2:i * 2 + 2], in1=expt[:, 0:2], op=ALU.pow)
            for j in range(2):
                ur = f_ur[j]
                urk = ("fur", j)
                P.op("dve", "tensor_scalar", reads=[("fxb", i, j), ("frstd", i)], writes=[urk], out=ur, in0=xb[:, j, :],
                     scalar1=f_rstd[:, i * 2 + j:i * 2 + j + 1], scalar2=None, op0=ALU.mult)
                b = nbank()
                pT = bank(b).bitcast(BF16).rearrange("p (c t) -> p c t", t=128)
                for c in range(8):
                    P.op("pe", "transpose", reads=[urk, "ident"], writes=[("ps", b)], out=pT[:, c, :],
                         in_=ur[:, c * 128:(c + 1) * 128], identity=ident)
                for c in range(8):
                    P.op("dve", "tensor_scalar", reads=[("ps", b), "vec"], writes=[("u2", i, j, c)],
                         out=u2[:, c, j * 128:(j + 1) * 128], in0=pT[:, c, :], scalar1=vec[:, 8 + c:9 + c], scalar2=None,
                         op0=ALU.mult)

        def f_up(tb):
            i = tb % 2
            u2, aT = u2s[i], aTs[i]
            u2keys = [("u2", i, j, c) for j in range(2) for c in range(8)]
            for fc in range(32):
                b = nbank()
                for k in range(8):
                    mm(bank(b)[:, 0:256], Wu[:, k, fc * 128:(fc + 1) * 128], u2[:, k, :], k == 0, k == 7,
                       reads=(u2keys if k in (0, 7) else []) + [("Wu", fc // 8)], writes=[("ps", b)])
                r = f_r[fc % 4]
                P.op("act", "activation", reads=[("ps", b)], writes=[("fr", fc % 4)], out=r, in_=bank(b)[:, 0:256], func=AF.Relu)
                eng = "dve" if fc % 2 == 0 else "pool"
                P.op(eng, "tensor_tensor", reads=[("fr", fc % 4)], writes=[("aT", i, fc)], out=aT[:, fc, :], in0=r, in1=r,
                     op=ALU.mult)

        def f_down(tb):
            i = tb % 2
            xb, aT = xbs[i], aTs[i]
            for j in range(2):
                pr = npair()
                for hf in range(2):
                    for fc in range(32):
                        mm(PS[pr][:, hf * 512:(hf + 1) * 512], aT[:, fc, j * 128:(j + 1) * 128],
                           Wd[:, fc, hf * 512:(hf + 1) * 512], fc == 0, fc == 31,
                           reads=[("aT", i, fc), ("Wd", fc // 8)], writes=[("ps", 2 * pr + hf)])
                for hf in range(2):
                    P.op("act", "activation", reads=[("ps", 2 * pr + hf)], writes=[("fss2", j, hf), "fjunk"], out=f_junk,
                         in_=PS[pr][:, hf * 512:(hf + 1) * 512], func=AF.Square, accum_out=f_ss2[:, hf * 2 + j:hf * 2 + j + 1])
                P.op("dve", "tensor_tensor", reads=[("fss2", j, 0), ("fss2", j, 1)], writes=[("fmse2", j)],
                     out=f_mse2[:, j:j + 1], in0=f_ss2[:, j:j + 1], in1=f_ss2[:, 2 + j:3 + j], op=ALU.add)
                P.op("dve", "tensor_scalar", reads=[("fmse2", j)], writes=[("fmse2", j)], out=f_mse2[:, j:j + 1],
                     in0=f_mse2[:, j:j + 1], scalar1=1.0 / D, scalar2=EPS, op0=ALU.mult, op1=ALU.add)
                P.op("pool", "tensor_tensor", reads=[("fmse2", j), "expt"], writes=[("frstd2", j)], out=f_rstd2[:, j:j + 1],
                     in0=f_mse2[:, j:j + 1], in1=expt[:, 0:1], op=ALU.pow)
                tmp = v32(F_T + 4 * K, 1024)
                P.op("dve", "tensor_tensor", reads=[("ps", 2 * pr), ("ps", 2 * pr + 1), ("gpost", 1)],
                     writes=[("fr", q) for q in range(4)], out=tmp, in0=PS[pr][:, :], in1=gpost[1], op=ALU.mult)
                P.op("dve", "scalar_tensor_tensor", reads=[("fr", q) for q in range(4)] + [("frstd2", j), ("fxb", i, j)],
                     writes=[("fxb", i, j)], out=xb[:, j, :], in0=tmp, scalar=f_rstd2[:, j:j + 1], in1=xb[:, j, :],
                     op0=ALU.mult, op1=ALU.add)
            P.dma("sp", P.group("fst%d" % i), x_dst_final[tb * 256:(tb + 1) * 256, :].rearrange("(j p) d -> p j d", p=128), xb,
                  reads=[("fxb", i, j) for j in range(2)], writes=[("xcur", tb // 2), ("xout", tb)])

        f_load(0)
        f_norm(0)
        for tb in range(FB):
            if tb + 1 < FB:
                f_load(tb + 1)
            f_up(tb)
            if tb + 1 < FB:
                f_norm(tb + 1)
            f_down(tb)

    P.barrier()
    P.op("sp", None)
    P.lower(stack)
    stack.close()
    return nc


def _perm(d):
    h = d // 2
    q = h // 2
    idx = np.arange(d)
    within = idx % h
    base = idx - within
    src = np.where(within < q, within + q, within - q) + base
    sign = np.where(within < q, -1.0, 1.0).astype(np.float32)
    return src, sign


def _rope_tables(rot_dim):
    rows = SEQ // 64
    row = np.repeat(np.arange(rows, dtype=np.float32), 64)
    col = np.tile(np.arange(64, dtype=np.float32), rows)
    half = rot_dim // 2
    inv = (np.float32(10000.0) ** (-np.arange(0, half, 2, dtype=np.float32) / np.float32(half))).astype(np.float32)
    ar = row[:, None] * inv[None, :]
    ac = col[:, None] * inv[None, :]
    ang = np.concatenate([ar, ar, ac, ac], axis=-1).astype(np.float32)
    return np.cos(ang).astype(np.float32), np.sin(ang).astype(np.float32)


_CACHE = {}


def _prep(inputs):
    f = lambda a: np.ascontiguousarray(np.asarray(a, dtype=np.float32))
    w_in = f(inputs["w_in"])
    L = w_in.shape[0]
    pA, sA = _perm(64)
    pB, sB = _perm(32)
    qcols = np.concatenate([np.concatenate([pc * 64 + np.arange(64), (4 + pc) * 64 + np.arange(64)]) for pc in range(4)])
    qpcols = np.concatenate([np.concatenate([pc * 64 + pA, (4 + pc) * 64 + pA]) for pc in range(4)])
    kcols = 512 + np.arange(128)
    kpcols = 512 + np.concatenate([pA, 64 + pA])
    cols = np.concatenate([qcols, qpcols, kcols, kpcols, 768 + np.arange(384), 1152 + np.arange(256),
                           1408 + np.arange(32), 1408 + pB, 640 + np.arange(128)])
    assert cols.shape[0] == NCOLA
    w_inA = np.ascontiguousarray(w_in[:, :, cols])
    w_g = np.ascontiguousarray(w_in[:, :, 1440:3488])
    wq = f(inputs["w_q_up"])
    qc = []
    for h in range(8):
        qc.append(h * 96 + np.arange(96))
        qc.append(np.concatenate([h * 96 + np.arange(64), h * 96 + 64 + pB]))
    w_qup = np.ascontiguousarray(wq[:, :, np.concatenate(qc)])
    wa = f(inputs["w_branch_a"])
    arow = np.concatenate([np.concatenate([pc * 64 + np.arange(64), (4 + pc) * 64 + np.arange(64)]) for pc in range(4)])
    w_a = np.ascontiguousarray(wa[:, arow, :])
    vecP = np.zeros((L, 128, 48), np.float32)
    vecP[:, :, 0:8] = f(inputs["pre_mix_g"]).reshape(L, 8, 128).transpose(0, 2, 1)
    vecP[:, :, 8:16] = f(inputs["pre_ffn_g"]).reshape(L, 8, 128).transpose(0, 2, 1)
    vecP[:, :, 16:19] = f(inputs["q_a_norm_g"]).reshape(L, 3, 128).transpose(0, 2, 1)
    vecP[:, :, 19:21] = f(inputs["kv_a_norm_g"]).reshape(L, 2, 128).transpose(0, 2, 1)
    qg = f(inputs["q_norm_g"])
    kg = f(inputs["k_norm_g"])
    vecP[:, :, 21] = np.concatenate([qg, qg], axis=1)
    vecP[:, :, 22] = np.concatenate([qg[:, pA], qg[:, pA]], axis=1)
    vecP[:, :, 23] = np.concatenate([kg, kg], axis=1)
    vecP[:, :, 24] = np.concatenate([kg[:, pA], kg[:, pA]], axis=1)
    vecP[:, :, 25:41] = f(inputs["b_gate"]).reshape(L, 16, 128).transpose(0, 2, 1)
    vecB = np.ascontiguousarray(np.stack([f(inputs["post_mix_g"]), f(inputs["post_ffn_g"])], axis=1))
    cA, sAt = _rope_tables(64)
    cB, sBt = _rope_tables(32)
    tabA_full = np.stack([np.concatenate([cA.T, cA.T], 0), np.concatenate([(sAt * sA[None, :]).T] * 2, 0)], 0)
    tabB_full = np.stack([cB.T, (sBt * sB[None, :]).T], 0)
    cst = np.zeros((3, 128, 128), np.float32)
    cst[0] = 1.0
    cst[1, 0:64, 0:64] = 1.0
    cst[1, 64:128, 64:128] = 1.0
    cst = cst.astype(ml_dtypes.bfloat16)
    ident = np.eye(128, dtype=np.float32).astype(ml_dtypes.bfloat16)
    shared = dict(w_inA=w_inA, w_g=w_g, w_qup=w_qup, w_kvup=f(inputs["w_kv_up"]), w_a=w_a, w_b=f(inputs["w_branch_b"]),
                  w_o=f(inputs["w_o"]), w_up=f(inputs["w_ffn_up"]), w_dn=f(inputs["w_ffn_down"]), vecP=vecP, vecB=vecB,
                  cst=cst, identd=ident)
    tabs = []
    for r in range(4):
        tabs.append(dict(tabA=np.ascontiguousarray(tabA_full[:, :, r * T:(r + 1) * T].astype(np.float32)),
                         tabB=np.ascontiguousarray(tabB_full[:, :, r * T:(r + 1) * T].astype(np.float32))))
    return shared, tabs, L


STOP = None


def _get_prog(depth):
    if depth not in _CACHE:
        _CACHE[depth] = build_program(depth, STOP)
    return _CACHE[depth]


def kernel(**inputs):
    x = np.ascontiguousarray(np.asarray(inputs["x"], dtype=np.float32))
    shared, tabs, L = _prep(inputs)
    per_layer = ("w_inA", "w_g", "w_qup", "w_kvup", "w_a", "w_b", "w_o", "w_up", "w_dn", "vecP", "vecB")
    xs = [np.ascontiguousarray(x[c // 4, (c % 4) * T:(c % 4 + 1) * T, :]) for c in range(8)]
    if FUSED:
        nc = _get_prog(L)
        in_maps = []
        for c in range(8):
            m = dict(shared)
            m.update(tabs[c % 4])
            m["x_in"] = xs[c]
            in_maps.append(m)
        res = run_bass_kernel_spmd(nc, in_maps, core_ids=list(range(8)))
        xs = [np.asarray(res.results[c]["out"]) for c in range(8)]
    else:
        nc = _get_prog(1)
        for l in range(L):
            in_maps = []
            for c in range(8):
                m = {k: (np.ascontiguousarray(v[l:l + 1]) if k in per_layer else v) for k, v in shared.items()}
                m.update(tabs[c % 4])
                m["x_in"] = xs[c]
                in_maps.append(m)
            res = run_bass_kernel_spmd(nc, in_maps, core_ids=list(range(8)))
            xs = [np.ascontiguousarray(np.asarray(res.results[c]["out"])) for c in range(8)]
    outp = np.empty_like(x)
    for c in range(8):
        outp[c // 4, (c % 4) * T:(c % 4 + 1) * T, :] = xs[c]
    return outp
```
